# Optimizing a Trainium2 kernel written in Bass

```python
import jax, jax.numpy as jnp
from jax import lax
import numpy as np

D_MODEL = 1024
BATCH = 1
SEQ = 16384
DEPTH = 4

CHUNK = 64
Q_BLOCK = 128
SGU_CHUNK = 128
HEAD_DIM = 64
D_ATTN = D_MODEL // 2
N_ATTN_HEADS = D_ATTN // HEAD_DIM
D_SGU = D_MODEL // 2
SGU_GROUP_DIM = 64
N_SGU_GROUPS = D_SGU // SGU_GROUP_DIM
D_MIX = D_ATTN + D_SGU
D_IN = 3 * D_ATTN + N_ATTN_HEADS + 2 * D_SGU
D_FF = -(-8 * D_MODEL // (3 * 256)) * 256
EPS = 1e-6

kernel_name = 'fox_gmlp_hybrid_trunk'


def rms_norm(x, g):
    xf = x.astype(jnp.float32)
    y = xf * lax.rsqrt(jnp.mean(xf * xf, axis=-1, keepdims=True) + EPS)
    return (y * g.astype(jnp.float32)).astype(x.dtype)


def layer_norm(x, g, b):
    xf = x.astype(jnp.float32)
    mu = jnp.mean(xf, axis=-1, keepdims=True)
    xc = xf - mu
    y = xc * lax.rsqrt(jnp.mean(xc * xc, axis=-1, keepdims=True) + EPS)
    return (y * g.astype(jnp.float32) + b.astype(jnp.float32)).astype(x.dtype)


def forgetting_attention(q, k, v, log_f):
    B, S, H, Dh = q.shape
    nb = S // Q_BLOCK
    c_bhs = jnp.cumsum(log_f, axis=1).transpose(0, 2, 1)
    q_blocks = q.reshape(B, nb, Q_BLOCK, H, Dh).transpose(1, 0, 2, 3, 4)
    cq_blocks = c_bhs.reshape(B, H, nb, Q_BLOCK).transpose(2, 0, 1, 3)
    starts = jnp.arange(nb, dtype=jnp.int32) * Q_BLOCK
    key_pos = jnp.arange(S, dtype=jnp.int32)
    scale = Dh ** -0.5

    def one_block(args):
        qb, cqb, start = args
        s = jnp.einsum('bqhd,bkhd->bhqk', qb, k).astype(jnp.float32) * scale
        s = s + cqb[..., :, None] - c_bhs[..., None, :]
        q_pos = start + jnp.arange(Q_BLOCK, dtype=jnp.int32)
        causal = key_pos[None, :] <= q_pos[:, None]
        s = jnp.where(causal, s, -jnp.inf)
        p = jax.nn.softmax(s, axis=-1).astype(v.dtype)
        return jnp.einsum('bhqk,bkhd->bqhd', p, v)

    out = lax.map(one_block, (q_blocks, cq_blocks, starts))
    return out.transpose(1, 0, 2, 3, 4).reshape(B, S, H * Dh)


def spatial_gating(z, ln_g, ln_b, w_s, b_s):
    B, S, _ = z.shape
    zu, zv = jnp.split(z, 2, axis=-1)
    zv = layer_norm(zv, ln_g, ln_b)
    nc = S // SGU_CHUNK
    zv = zv.reshape(B, nc, SGU_CHUNK, N_SGU_GROUPS, SGU_GROUP_DIM)
    pos = jnp.arange(SGU_CHUNK, dtype=jnp.int32) // CHUNK
    mask = (pos[None, :] <= pos[:, None]).astype(w_s.dtype)
    w = w_s * mask[None]
    mixed = jnp.einsum('gij,bcjgd->bcigd', w, zv) + b_s.T[None, None, :, :, None]
    return zu * mixed.reshape(B, S, D_SGU)


def setup_inputs(seed: int = 0) -> dict:
    key = jax.random.key(seed)
    ks = jax.random.split(key, 14)
    f32 = jnp.float32
    nrm = lambda k, shape, s: jax.random.normal(k, shape, f32) * s
    head_bias = jnp.linspace(1.0, 5.0, N_ATTN_HEADS, dtype=f32)
    return {
        'x': jax.random.normal(ks[0], (BATCH, SEQ, D_MODEL), f32),
        'mix_norm_g': 1.0 + nrm(ks[1], (DEPTH, D_MODEL), 0.05),
        'w_in': nrm(ks[2], (DEPTH, D_MODEL, D_IN), D_MODEL ** -0.5),
        'b_f': head_bias[None, :] + nrm(ks[3], (DEPTH, N_ATTN_HEADS), 0.1),
        'sgu_ln_g': 1.0 + nrm(ks[4], (DEPTH, D_SGU), 0.05),
        'sgu_ln_b': nrm(ks[5], (DEPTH, D_SGU), 0.02),
        'w_s': nrm(ks[6], (DEPTH, N_SGU_GROUPS, SGU_CHUNK, SGU_CHUNK), 0.5 * SGU_CHUNK ** -0.5),
        'b_s': 1.0 + nrm(ks[7], (DEPTH, N_SGU_GROUPS, SGU_CHUNK), 0.1),
        'out_norm_g': 1.0 + nrm(ks[8], (DEPTH, D_MIX), 0.05),
        'w_out': nrm(ks[9], (DEPTH, D_MIX, D_MODEL), D_MIX ** -0.5),
        'ffn_norm_g': 1.0 + nrm(ks[10], (DEPTH, D_MODEL), 0.05),
        'w_gate_up': nrm(ks[11], (DEPTH, D_MODEL, 2 * D_FF), D_MODEL ** -0.5),
        'w_down': nrm(ks[12], (DEPTH, D_FF, D_MODEL), D_FF ** -0.5),
        'final_norm_g': 1.0 + nrm(ks[13], (D_MODEL,), 0.05),
    }


def reference(x, mix_norm_g, w_in, b_f, sgu_ln_g, sgu_ln_b, w_s, b_s, out_norm_g, w_out,
              ffn_norm_g, w_gate_up, w_down, final_norm_g):
    B, S, _ = x.shape
    for l in range(DEPTH):
        xn = rms_norm(x, mix_norm_g[l])
        h = xn @ w_in[l]
        q, k, v, f_logit, z = jnp.split(
            h, [D_ATTN, 2 * D_ATTN, 3 * D_ATTN, 3 * D_ATTN + N_ATTN_HEADS], axis=-1)
        q = q.reshape(B, S, N_ATTN_HEADS, HEAD_DIM)
        k = k.reshape(B, S, N_ATTN_HEADS, HEAD_DIM)
        v = v.reshape(B, S, N_ATTN_HEADS, HEAD_DIM)
        log_f = jax.nn.log_sigmoid(f_logit.astype(jnp.float32) + b_f[l].astype(jnp.float32))
        attn = forgetting_attention(q, k, v, log_f)
        sgu = spatial_gating(jax.nn.gelu(z, approximate=False),
                             sgu_ln_g[l], sgu_ln_b[l], w_s[l], b_s[l])
        merged = jnp.concatenate(
            [rms_norm(attn, out_norm_g[l, :D_ATTN]), rms_norm(sgu, out_norm_g[l, D_ATTN:])], axis=-1)
        x = x + merged @ w_out[l]
        xn = rms_norm(x, ffn_norm_g[l])
        gate, up = jnp.split(xn @ w_gate_up[l], 2, axis=-1)
        x = x + (jax.nn.silu(gate) * up) @ w_down[l]
    return rms_norm(x, final_norm_g)
```

```python
import numpy as np
import ml_dtypes
import concourse.bass as bass
import concourse.mybir as mybir
from concourse.bass_utils import run_bass_kernel_spmd

F32 = mybir.dt.float32
BF16 = mybir.dt.bfloat16
AF = mybir.ActivationFunctionType
ALU = mybir.AluOpType

NCORES = 8
D = 1024
S = 16384
DEPTH = 4
TS = S // NCORES
DH = 64
NH = 8
D_IN = 2568
DFF = 2816
EPS = 1e-6
NEG = -30000.0


class Sched:
    ENG = ("pe", "act", "dve", "pool", "sp")

    def __init__(self, nc):
        self.nc = nc
        self.ops = []
        self.by_eng = {e: [] for e in self.ENG}
        self.last_w = {}
        self.readers = {}
        self.slot_count = {}

    def op(self, eng, fn, reads=(), writes=(), dma_slot=None):
        idx = len(self.ops)
        deps = set()
        for r in reads:
            if r in self.last_w:
                deps.add(self.last_w[r])
        for w in writes:
            if w in self.last_w:
                deps.add(self.last_w[w])
            for rd in self.readers.get(w, ()):
                deps.add(rd)
        deps.discard(idx)
        rec = dict(eng=eng, fn=fn, deps=deps, dma=dma_slot, signal=False, ordinal=None)
        if dma_slot is not None:
            self.slot_count[dma_slot] = self.slot_count.get(dma_slot, 0) + 1
            rec["ordinal"] = self.slot_count[dma_slot]
        self.ops.append(rec)
        self.by_eng[eng].append(idx)
        for r in reads:
            self.readers.setdefault(r, []).append(idx)
        for w in writes:
            self.last_w[w] = idx
            self.readers[w] = []
        return idx

    def dma(self, q, out, in_, reads=(), writes=(), slot=None, **kw):
        assert slot is not None
        return self.op(q, lambda e: e.dma_start(out=out, in_=in_, **kw), reads, writes, dma_slot=slot)

    def emit(self):
        nc = self.nc
        ops = self.ops
        for i, o in enumerate(ops):
            keep = set()
            for d in o["deps"]:
                od = ops[d]
                if od["dma"] is None and od["eng"] == o["eng"] and o["dma"] is None and o["eng"] == "pe":
                    continue
                keep.add(d)
                if od["dma"] is None:
                    od["signal"] = True
            o["deps"] = keep
        sigcount = {}
        for e in self.ENG:
            c = 0
            for i in self.by_eng[e]:
                if ops[i]["dma"] is None and ops[i]["signal"]:
                    c += 1
                    sigcount[i] = c
        import contextlib
        with contextlib.ExitStack() as st:
            esem = {e: st.enter_context(nc.semaphore("s_" + e)) for e in self.ENG}
            ssem = {s: st.enter_context(nc.semaphore("d_%d" % k)) for k, s in enumerate(self.slot_count)}
            block = st.enter_context(nc.Block())

            def run(ename, eng):
                waited = {}
                for i in self.by_eng[ename]:
                    o = ops[i]
                    need = {}
                    for d in o["deps"]:
                        od = ops[d]
                        if od["dma"] is not None:
                            key = ("d", od["dma"])
                            val = 16 * od["ordinal"]
                        else:
                            key = ("e", od["eng"])
                            val = sigcount[d]
                        if val > need.get(key, 0):
                            need[key] = val
                    for key, val in need.items():
                        if waited.get(key, 0) >= val:
                            continue
                        waited[key] = val
                        sem = ssem[key[1]] if key[0] == "d" else esem[key[1]]
                        eng.wait_ge(sem, val)
                    ins = o["fn"](eng)
                    if o["dma"] is not None:
                        ins.then_inc(ssem[o["dma"]], 16)
                    elif o["signal"]:
                        ins.then_inc(esem[ename], 1)
                if ename == "sp":
                    for s, n in self.slot_count.items():
                        eng.wait_ge(ssem[s], 16 * n)

            @block.tensor
            def _(e):
                run("pe", e)

            @block.scalar
            def _(e):
                run("act", e)

            @block.vector
            def _(e):
                run("dve", e)

            @block.gpsimd
            def _(e):
                run("pool", e)

            @block.sync
            def _(e):
                run("sp", e)


def _new_nc():
    return bass.Bass("TRN2", target_bir_lowering=False)


def _din(nc, name, shape, dt):
    return nc.dram_tensor(name, list(shape), dt, kind="ExternalInput").ap()


def _dout(nc, name, shape, dt):
    return nc.dram_tensor(name, list(shape), dt, kind="ExternalOutput").ap()


def _sb(nc, name, shape, dt):
    return nc.alloc_sbuf_tensor(name, list(shape), dt)


def _ps(nc, name, shape, dt=F32):
    return nc.alloc_psum_tensor(name, list(shape), dt)


def build_A():
    nc = _new_nc()
    xT = _din(nc, "xT", [D, TS], F32)
    gmix = _din(nc, "gmix", [128, 8], F32)
    w_in = _din(nc, "w_in", [D, D_IN], F32)
    lng = _din(nc, "lng", [128, 512], F32)
    lnb = _din(nc, "lnb", [128, 512], F32)
    wsT = _din(nc, "wsT", [128, 8, 128], F32)
    bsb = _din(nc, "bsb", [128, 4, 128], F32)
    gos = _din(nc, "gos", [128, 4], F32)
    identf = _din(nc, "identf", [128, 128], F32)
    qkvT = _dout(nc, "qkvT", [1536, TS], BF16)
    fT = _dout(nc, "fT", [8, TS], F32)
    sgun = _dout(nc, "sgun", [512, TS], BF16)

    X = _sb(nc, "X", [128, 8, TS], F32)
    W = _sb(nc, "W", [128, 8, D_IN], BF16)
    G = _sb(nc, "G", [128, 8], F32)
    LNG = _sb(nc, "LNG", [128, 512], F32)
    LNB = _sb(nc, "LNB", [128, 512], F32)
    WS = _sb(nc, "WS", [128, 8, 128], BF16)
    BS = _sb(nc, "BS", [128, 4, 128], F32)
    GO = _sb(nc, "GO", [128, 4], F32)
    IDF = _sb(nc, "IDF", [128, 128], F32)
    ONES = _sb(nc, "ONES", [128, 128], BF16)
    EPSC = _sb(nc, "EPSC", [128, 1], F32)
    SQ = _sb(nc, "SQ", [128, 8, 512], BF16)
    RSQ = _sb(nc, "RSQ", [128, 512], F32)
    RSTD = _sb(nc, "RSTD", [128, 512], F32)
    XB = _sb(nc, "XB", [128, 8, 512], BF16)
    STG = [_sb(nc, "STG%d" % i, [128, 512], BF16) for i in range(3)]
    FST = _sb(nc, "FST", [8, 512], F32)
    ZR = [_sb(nc, "ZR%d" % i, [128, 512], F32) for i in range(2)]
    ZU = _sb(nc, "ZU", [128, 4, 512], F32)
    ZV = _sb(nc, "ZV", [128, 4, 512], F32)
    ST6 = _sb(nc, "ST6", [128, 6], F32)
    MV = _sb(nc, "MV", [128, 2], F32)
    SD = _sb(nc, "SD", [128, 1], F32)
    RL = _sb(nc, "RL", [128, 1], F32)
    ZN = _sb(nc, "ZN", [128, 512], F32)
    ZN2 = _sb(nc, "ZN2", [128, 512], F32)
    ZNB = _sb(nc, "ZNB", [128, 512], BF16)
    SG1 = _sb(nc, "SG1", [128, 4, 128], F32)
    SGU = _sb(nc, "SGU", [128, 4, 512], F32)
    SQ2 = _sb(nc, "SQ2", [128, 4, 512], BF16)
    RSQ2 = _sb(nc, "RSQ2", [128, 512], F32)
    RS2 = _sb(nc, "RS2", [128, 512], F32)
    SGN = _sb(nc, "SGN", [128, 4, 512], BF16)

    P_SS = _ps(nc, "P_SS", [128, 512])
    P_PJ = [_ps(nc, "P_PJ%d" % i, [128, 512]) for i in range(2)]
    P_ZT = _ps(nc, "P_ZT", [128, 512])
    P_MX = _ps(nc, "P_MX", [128, 4, 128])
    P_S2 = _ps(nc, "P_S2", [128, 512])

    s = Sched(nc)
    s.dma("sp", G[:, :], gmix, writes=["G"], slot="G")
    s.dma("sp", LNG[:, :], lng, writes=["LNG"], slot="LNG")
    s.dma("sp", LNB[:, :], lnb, writes=["LNB"], slot="LNB")
    s.dma("sp", BS[:, :, :], bsb, writes=["BS"], slot="BS")
    s.dma("sp", GO[:, :], gos, writes=["GO"], slot="GO")
    s.dma("sp", IDF[:, :], identf, writes=["IDF"], slot="IDF")
    xT_v = xT.rearrange("(c p) t -> p c t", p=128)
    for tg in range(4):
        s.dma("sp", X[:, :, tg * 512:(tg + 1) * 512], xT_v[:, :, tg * 512:(tg + 1) * 512],
              writes=["X%d" % tg], slot="X%d" % tg)
    w_v = w_in.rearrange("(c p) n -> p c n", p=128)
    panels = [(0, 512), (512, 1024), (1024, 1544), (1544, 2056), (2056, 2568)]
    for pi, (a, b) in enumerate(panels):
        s.dma("pool", W[:, :, a:b], w_v[:, :, a:b], writes=["W%d" % pi], slot="W%d" % pi)
    s.dma("pool", WS[:, :, :], wsT, writes=["WS"], slot="WS")
    s.op("pool", lambda e: e.memset(WS[64:128, :, 0:64], 0.0), reads=[], writes=["WS"])
    s.op("dve", lambda e: e.memset(ONES[:, :], 1.0), writes=["ONES"])
    s.op("dve", lambda e: e.memset(EPSC[:, :], EPS), writes=["EPSC"])

    def wpanel(col):
        for pi, (a, b) in enumerate(panels):
            if a <= col < b:
                return "W%d" % pi
        raise ValueError

    pj = [0]
    stg = [0]

    for tg in range(4):
        t0 = tg * 512
        xs = X[:, :, t0:t0 + 512]
        xr = "X%d" % tg
        s.op("act", lambda e, xs=xs: e.activation(out=SQ[:, :, :], in_=xs, func=AF.Square),
             reads=[xr], writes=["SQ"])
        for c in range(8):
            s.op("pe", lambda e, c=c: e.matmul(P_SS[:, :], lhsT=ONES[:, :], rhs=SQ[:, c, :],
                                               start=(c == 0), stop=(c == 7)),
                 reads=["ONES", "SQ"], writes=["P_SS"])
        s.op("act", lambda e: e.activation(out=RSQ[:, :], in_=P_SS[:, :], func=AF.Sqrt,
                                           bias=EPSC[:, 0:1], scale=1.0 / D),
             reads=["P_SS", "EPSC"], writes=["RSQ"])
        s.op("dve", lambda e: e.reciprocal(out=RSTD[:, :], in_=RSQ[:, :]), reads=["RSQ"], writes=["RSTD"])
        for c in range(8):
            eng = "dve" if c % 2 == 0 else "pool"
            s.op(eng, lambda e, c=c, xs=xs: e.tensor_scalar(out=XB[:, c, :], in0=xs[:, c, :],
                                                           scalar1=G[:, c:c + 1], scalar2=None, op0=ALU.mult),
                 reads=[xr, "G"], writes=["XB%d" % c])
        xbr = ["XB%d" % c for c in range(8)]

        for oc in range(12):
            pb = pj[0] % 2
            pj[0] += 1
            P = P_PJ[pb]
            for c in range(8):
                s.op("pe", lambda e, c=c, oc=oc, P=P: e.matmul(P[:, :], lhsT=W[:, c, oc * 128:(oc + 1) * 128],
                                                             rhs=XB[:, c, :], start=(c == 0), stop=(c == 7)),
                     reads=[wpanel(oc * 128)] + xbr, writes=["P_PJ%d" % pb])
            sb = stg[0] % 3
            stg[0] += 1
            sc = 0.125 if oc < 4 else 1.0
            s.op("dve", lambda e, P=P, sb=sb, sc=sc: e.scalar_tensor_tensor(
                out=STG[sb][:, :], in0=P[:, :], scalar=sc, in1=RSTD[:, :], op0=ALU.mult, op1=ALU.mult),
                 reads=["P_PJ%d" % pb, "RSTD"], writes=["STG%d" % sb])
            s.dma("sp", qkvT[oc * 128:(oc + 1) * 128, t0:t0 + 512], STG[sb][:, :],
                  reads=["STG%d" % sb], writes=["o_qkv"], slot="STG%d" % sb)
        pb = pj[0] % 2
        pj[0] += 1
        P = P_PJ[pb]
        for c in range(8):
            s.op("pe", lambda e, c=c, P=P: e.matmul(P[0:8, :], lhsT=W[:, c, 1536:1544], rhs=XB[:, c, :],
                                                   start=(c == 0), stop=(c == 7)),
                 reads=[wpanel(1536)] + xbr, writes=["P_PJ%d" % pb])
        s.op("dve", lambda e, P=P: e.tensor_tensor(out=FST[:, :], in0=P[0:8, :], in1=RSTD[0:8, :], op=ALU.mult),
             reads=["P_PJ%d" % pb, "RSTD"], writes=["FST"])
        s.dma("sp", fT[:, t0:t0 + 512], FST[:, :], reads=["FST"], writes=["o_f"], slot="FST")
        for zc in range(8):
            pb = pj[0] % 2
            pj[0] += 1
            P = P_PJ[pb]
            col = 1544 + zc * 128
            for c in range(8):
                s.op("pe", lambda e, c=c, col=col, P=P: e.matmul(P[:, :], lhsT=W[:, c, col:col + 128],
                                                               rhs=XB[:, c, :], start=(c == 0), stop=(c == 7)),
                     reads=[wpanel(col), wpanel(col + 127)] + xbr, writes=["P_PJ%d" % pb])
            zb = zc % 2
            s.op("dve", lambda e, P=P, zb=zb: e.tensor_tensor(out=ZR[zb][:, :], in0=P[:, :], in1=RSTD[:, :],
                                                            op=ALU.mult),
                 reads=["P_PJ%d" % pb, "RSTD"], writes=["ZR%d" % zb])
            dst = ZU[:, zc, :] if zc < 4 else ZV[:, zc - 4, :]
            dr = ("ZU%d" % zc) if zc < 4 else ("ZV%d" % (zc - 4))
            s.op("act", lambda e, zb=zb, dst=dst: e.activation(out=dst, in_=ZR[zb][:, :], func=AF.Gelu),
                 reads=["ZR%d" % zb], writes=[dr])
        zur = ["ZU%d" % c for c in range(4)]
        zvr = ["ZV%d" % c for c in range(4)]
        for tt in range(4):
            a0 = tt * 128
            for c4 in range(4):
                s.op("pe", lambda e, c4=c4, a0=a0: e.transpose(out=P_ZT[:, c4 * 128:(c4 + 1) * 128],
                                                             in_=ZV[:, c4, a0:a0 + 128], identity=IDF[:, :]),
                     reads=["ZV%d" % c4, "IDF"], writes=["P_ZT"])
            s.op("dve", lambda e: e.bn_stats(out=ST6[:, :], in_=P_ZT[:, :]), reads=["P_ZT"], writes=["ST6"])
            s.op("dve", lambda e: e.bn_aggr(out=MV[:, :], in_=ST6[:, :]), reads=["ST6"], writes=["MV"])
            s.op("act", lambda e: e.activation(out=SD[:, :], in_=MV[:, 1:2], func=AF.Sqrt, bias=EPSC[:, 0:1],
                                               scale=1.0),
                 reads=["MV", "EPSC"], writes=["SD"])
            s.op("dve", lambda e: e.reciprocal(out=RL[:, :], in_=SD[:, :]), reads=["SD"], writes=["RL"])
            s.op("dve", lambda e: e.tensor_scalar(out=ZN[:, :], in0=P_ZT[:, :], scalar1=MV[:, 0:1],
                                                  scalar2=RL[:, 0:1], op0=ALU.subtract, op1=ALU.mult),
                 reads=["P_ZT", "MV", "RL"], writes=["ZN"])
            s.op("pool", lambda e: e.tensor_tensor(out=ZN2[:, :], in0=ZN[:, :], in1=LNG[:, :], op=ALU.mult),
                 reads=["ZN", "LNG"], writes=["ZN2"])
            s.op("dve", lambda e: e.tensor_tensor(out=ZNB[:, :], in0=ZN2[:, :], in1=LNB[:, :], op=ALU.add),
                 reads=["ZN2", "LNB"], writes=["ZNB"])
            for g in range(8):
                po = (g % 2) * 64
                s.op("pe", lambda e, g=g, po=po: e.matmul(P_MX[po:po + 64, g // 2, :],
                                                        lhsT=ZNB[:, g * 64:(g + 1) * 64], rhs=WS[:, g, :],
                                                        start=True, stop=True),
                     reads=["ZNB", "WS"], writes=["P_MX"])
            s.op("dve", lambda e: e.tensor_tensor(out=SG1[:, :, :], in0=P_MX[:, :, :], in1=BS[:, :, :], op=ALU.add),
                 reads=["P_MX", "BS"], writes=["SG1"])
            s.op("pool", lambda e, a0=a0: e.tensor_tensor(out=SGU[:, :, a0:a0 + 128], in0=SG1[:, :, :],
                                                        in1=ZU[:, :, a0:a0 + 128], op=ALU.mult),
                 reads=["SG1"] + zur, writes=["SGU"])
        s.op("act", lambda e: e.activation(out=SQ2[:, :, :], in_=SGU[:, :, :], func=AF.Square),
             reads=["SGU"], writes=["SQ2"])
        for c in range(4):
            s.op("pe", lambda e, c=c: e.matmul(P_S2[:, :], lhsT=ONES[:, :], rhs=SQ2[:, c, :],
                                               start=(c == 0), stop=(c == 3)),
                 reads=["ONES", "SQ2"], writes=["P_S2"])
        s.op("act", lambda e: e.activation(out=RSQ2[:, :], in_=P_S2[:, :], func=AF.Sqrt, bias=EPSC[:, 0:1],
                                           scale=1.0 / 512),
             reads=["P_S2", "EPSC"], writes=["RSQ2"])
        s.op("dve", lambda e: e.reciprocal(out=RS2[:, :], in_=RSQ2[:, :]), reads=["RSQ2"], writes=["RS2"])
        for c in range(4):
            s.op("dve", lambda e, c=c: e.scalar_tensor_tensor(out=SGN[:, c, :], in0=SGU[:, c, :],
                                                            scalar=GO[:, c:c + 1], in1=RS2[:, :],
                                                            op0=ALU.mult, op1=ALU.mult),
                 reads=["SGU", "GO", "RS2"], writes=["SGN"])
        s.dma("sp", sgun.rearrange("(c p) t -> p c t", p=128)[:, :, t0:t0 + 512], SGN[:, :, :],
              reads=["SGN"], writes=["o_sg"], slot="SGN")
    s.emit()
    return nc


def build_B():
    nc = _new_nc()
    qT = _din(nc, "qT", [DH, S], BF16)
    kT = _din(nc, "kT", [DH, S], BF16)
    v = _din(nc, "v", [S, DH], BF16)
    f = _din(nc, "f", [128, 128], F32)
    bf = _din(nc, "bf", [128, 1], F32)
    identf = _din(nc, "identf", [128, 128], F32)
    identb = _din(nc, "identb", [128, 128], BF16)
    tri = _din(nc, "tri", [128, 128], F32)
    maskneg = _din(nc, "maskneg", [128, 4, 512], BF16)
    oT = _dout(nc, "oT", [DH, S], F32)
    crow = nc.dram_tensor("crow", [3, S], BF16)

    QA = _sb(nc, "QA", [67, S], BF16)
    KA = _sb(nc, "KA", [67, S], BF16)
    V = _sb(nc, "V", [128, 128, 65], BF16)
    IDF = _sb(nc, "IDF", [128, 128], F32)
    IDB = _sb(nc, "IDB", [128, 128], BF16)
    TRI = _sb(nc, "TRI", [128, 128], F32)
    MN = _sb(nc, "MN", [128, 4, 512], BF16)
    ONEF = _sb(nc, "ONEF", [128, 128], F32)
    Fb = _sb(nc, "Fb", [128, 128], F32)
    BFb = _sb(nc, "BFb", [128, 1], F32)
    Y = _sb(nc, "Y", [128, 128], F32)
    A = _sb(nc, "A", [128, 128], F32)
    E = _sb(nc, "E", [128, 128], F32)
    L = _sb(nc, "L", [128, 128], F32)
    M = _sb(nc, "M", [128, 128], F32)
    LF = _sb(nc, "LF", [128, 128], F32)
    SC = _sb(nc, "SC", [128, 128], F32)
    OFF = _sb(nc, "OFF", [128, 2], F32)
    C = _sb(nc, "C", [128, 128], F32)
    NEGC = _sb(nc, "NEGC", [128, 128], F32)
    HI = _sb(nc, "HI", [128, 3, 128], BF16)
    HF = _sb(nc, "HF", [128, 128], F32)
    R1 = _sb(nc, "R1", [128, 128], F32)
    R2 = _sb(nc, "R2", [128, 128], F32)
    PT = [_sb(nc, "PT%d" % i, [128, 1024], BF16) for i in range(3)]
    OSB = [_sb(nc, "OSB%d" % i, [65, 512], F32) for i in range(2)]
    RDEN = [_sb(nc, "RDEN%d" % i, [65, 512], F32) for i in range(2)]
    OTS = [_sb(nc, "OTS%d" % i, [64, 512], F32) for i in range(2)]

    PS_S = [_ps(nc, "PS_S%d" % i, [128, 1024]) for i in range(2)]
    PS_O = [[_ps(nc, "PS_O%d%d" % (i, j), [128, 512]) for j in range(2)] for i in range(2)]

    s = Sched(nc)
    s.dma("sp", Fb[:, :], f, writes=["Fb"], slot="Fb")
    s.dma("sp", BFb[:, :], bf, writes=["BFb"], slot="BFb")
    s.dma("sp", IDF[:, :], identf, writes=["IDF"], slot="IDF")
    s.dma("sp", IDB[:, :], identb, writes=["IDB"], slot="IDB")
    s.dma("sp", TRI[:, :], tri, writes=["TRI"], slot="TRI")
    s.dma("sp", MN[:, :, :], maskneg, writes=["MN"], slot="MN")
    for i in range(4):
        s.dma("sp", KA[0:64, i * 4096:(i + 1) * 4096], kT[:, i * 4096:(i + 1) * 4096], writes=["KA"], slot="KA%d" % i)
        s.dma("sp", QA[0:64, i * 4096:(i + 1) * 4096], qT[:, i * 4096:(i + 1) * 4096], writes=["QAq"], slot="QA%d" % i)
    v_v = v.rearrange("(b p) d -> p b d", p=128)
    for i in range(4):
        s.dma("sp", V[:, i * 32:(i + 1) * 32, 0:64], v_v[:, i * 32:(i + 1) * 32, :], writes=["V"], slot="V%d" % i)
    s.op("pool", lambda e: e.memset(V[:, :, 64:65], 1.0), writes=["V"])
    s.op("pool", lambda e: e.memset(KA[64:67, :], 1.0), writes=["KA"])
    s.op("pool", lambda e: e.memset(ONEF[:, :], 1.0), writes=["ONEF"])

    s.op("dve", lambda e: e.tensor_scalar(out=Y[:, :], in0=Fb[:, :], scalar1=BFb[:, 0:1], scalar2=-1.0,
                                          op0=ALU.add, op1=ALU.mult), reads=["Fb", "BFb"], writes=["Y"])
    s.op("dve", lambda e: e.tensor_scalar(out=E[:, :], in0=Y[:, :], scalar1=-1.0, scalar2=None, op0=ALU.mult),
         reads=["Y"], writes=["E"])
    s.op("dve", lambda e: e.tensor_tensor(out=A[:, :], in0=Y[:, :], in1=E[:, :], op=ALU.max),
         reads=["Y", "E"], writes=["A"])
    s.op("act", lambda e: e.activation(out=E[:, :], in_=A[:, :], func=AF.Exp, scale=-1.0), reads=["A"], writes=["E"])
    s.op("act", lambda e: e.activation(out=L[:, :], in_=E[:, :], func=AF.Ln, bias=ONEF[:, 0:1], scale=1.0),
         reads=["E", "ONEF"], writes=["L"])
    s.op("dve", lambda e: e.tensor_single_scalar(out=M[:, :], in_=Y[:, :], scalar=0.0, op=ALU.max),
         reads=["Y"], writes=["M"])
    s.op("dve", lambda e: e.scalar_tensor_tensor(out=LF[:, :], in0=M[:, :], scalar=-1.0, in1=L[:, :],
                                                 op0=ALU.mult, op1=ALU.subtract), reads=["M", "L"], writes=["LF"])
    s.op("dve", lambda e: e.tensor_tensor_scan(out=SC[:, :], data0=ONEF[:, :], data1=LF[:, :], initial=0.0,
                                               op0=ALU.mult, op1=ALU.add), reads=["ONEF", "LF"], writes=["SC"])
    PO = PS_O[0][0]
    s.op("pe", lambda e: e.matmul(PO[:, 0:2], lhsT=TRI[:, :], rhs=SC[:, 126:128], start=True, stop=True),
         reads=["TRI", "SC"], writes=["PS_O00"])
    s.op("dve", lambda e: e.tensor_copy(out=OFF[:, :], in_=PO[:, 0:2]), reads=["PS_O00"], writes=["OFF"])
    s.op("dve", lambda e: e.tensor_scalar(out=C[:, :], in0=SC[:, :], scalar1=OFF[:, 1:2], scalar2=None, op0=ALU.add),
         reads=["SC", "OFF"], writes=["C"])
    PO2 = PS_O[0][1]
    s.op("pe", lambda e: e.transpose(out=PO2[:, 0:128], in_=C[:, :], identity=IDF[:, :]),
         reads=["C", "IDF"], writes=["PS_O01"])
    s.op("dve", lambda e: e.tensor_scalar(out=NEGC[:, :], in0=PO2[:, 0:128], scalar1=-1.0, scalar2=None, op0=ALU.mult),
         reads=["PS_O01"], writes=["NEGC"])
    s.op("dve", lambda e: e.tensor_copy(out=HI[:, 0, :], in_=C[:, :]), reads=["C"], writes=["HI0"])
    s.op("dve", lambda e: e.tensor_copy(out=HF[:, :], in_=HI[:, 0, :]), reads=["HI0"], writes=["HF"])
    s.op("dve", lambda e: e.tensor_tensor(out=R1[:, :], in0=C[:, :], in1=HF[:, :], op=ALU.subtract),
         reads=["C", "HF"], writes=["R1"])
    s.op("dve", lambda e: e.tensor_copy(out=HI[:, 1, :], in_=R1[:, :]), reads=["R1"], writes=["HI1"])
    s.op("dve", lambda e: e.tensor_copy(out=HF[:, :], in_=HI[:, 1, :]), reads=["HI1"], writes=["HF"])
    s.op("dve", lambda e: e.tensor_tensor(out=R2[:, :], in0=R1[:, :], in1=HF[:, :], op=ALU.subtract),
         reads=["R1", "HF"], writes=["R2"])
    s.op("dve", lambda e: e.tensor_copy(out=HI[:, 2, :], in_=R2[:, :]), reads=["R2"], writes=["HI2"])
    crow_ap = crow.ap()
    for r in range(3):
        s.dma("sp", crow_ap[r:r + 1, :].rearrange("o (p j) -> (o p) j", p=128), HI[:, r, :],
              reads=["HI%d" % r], writes=["crow"], slot="crow%d" % r)
    s.dma("sp", QA[64:67, :], crow_ap[:, :], reads=["crow"], writes=["QAc"], slot="QAc")

    items = []
    for P in range(16):
        for kb in range(8 * P + 8):
            items.append((P, kb))
    n = len(items)
    pending_evac = []

    def emit_qk(it):
        P, kb = items[it]
        sb = it % 2
        SS = PS_S[sb]
        subs = [0, 1] if kb <= 8 * P + 3 else [1]
        for sub in subs:
            q0 = P * 1024 + sub * 512
            kbd = kb - (8 * P + 4 * sub)
            diag = 0 <= kbd <= 3
            s.op("pe", lambda e, SS=SS, sub=sub, q0=q0, kb=kb, diag=diag: e.matmul(
                SS[:, sub * 512:(sub + 1) * 512], lhsT=KA[0:67, kb * 128:(kb + 1) * 128], rhs=QA[0:67, q0:q0 + 512],
                start=True, stop=(not diag)), reads=["KA", "QAq", "QAc"], writes=["PS_S%d" % sb])
            if diag:
                s.op("pe", lambda e, SS=SS, sub=sub, kbd=kbd: e.matmul(
                    SS[:, sub * 512:(sub + 1) * 512], lhsT=IDB[:, :], rhs=MN[:, kbd, :], start=False, stop=True),
                    reads=["IDB", "MN"], writes=["PS_S%d" % sb])
        lo = subs[0] * 512
        pb = it % 3
        s.op("act", lambda e, SS=SS, lo=lo, pb=pb, kb=kb: e.activation(
            out=PT[pb][:, lo:1024], in_=SS[:, lo:1024], func=AF.Exp, bias=NEGC[:, kb:kb + 1], scale=1.0),
            reads=["PS_S%d" % sb, "NEGC"], writes=["PT%d" % pb])

    def emit_pv(it):
        P, kb = items[it]
        ob = P % 2
        pb = it % 3
        subs = [0, 1] if kb <= 8 * P + 3 else [1]
        for sub in subs:
            last = 8 * P + 4 * sub + 3
            PO_ = PS_O[ob][sub]
            s.op("pe", lambda e, PO_=PO_, kb=kb, pb=pb, sub=sub, last=last: e.matmul(
                PO_[0:65, :], lhsT=V[:, kb, 0:65], rhs=PT[pb][:, sub * 512:(sub + 1) * 512],
                start=(kb == 0), stop=(kb == last)), reads=["V", "PT%d" % pb], writes=["PS_O%d%d" % (ob, sub)])
            if kb == last:
                pending_evac.append([P, sub, it + 3, 0])

    def emit_evac(P, sub, stage):
        ob = P % 2
        PO_ = PS_O[ob][sub]
        por = "PS_O%d%d" % (ob, sub)
        q0 = P * 1024 + sub * 512
        if stage == 0:
            s.op("dve", lambda e: e.tensor_copy(out=OSB[sub][:, :], in_=PO_[0:65, :]), reads=[por],
                 writes=["OSB%d" % sub])
            s.op("dve", lambda e: e.reciprocal(out=RDEN[sub][64:65, :], in_=OSB[sub][64:65, :]),
                 reads=["OSB%d" % sub], writes=["RDEN%d" % sub])
        else:
            s.op("pe", lambda e: e.matmul(PO_[0:64, :], lhsT=ONEF[64:65, 0:64], rhs=RDEN[sub][64:65, :],
                                          start=True, stop=True), reads=["ONEF", "RDEN%d" % sub], writes=[por])
            s.op("dve", lambda e: e.tensor_tensor(out=OTS[sub][:, :], in0=OSB[sub][0:64, :], in1=PO_[0:64, :],
                                                  op=ALU.mult), reads=["OSB%d" % sub, por], writes=["OTS%d" % sub])
            s.dma("sp", oT[:, q0:q0 + 512], OTS[sub][:, :], reads=["OTS%d" % sub], writes=["o_o"],
                  slot="OTS%d" % sub)

    for it in range(n + 1):
        if it < n:
            emit_qk(it)
        if it >= 1:
            emit_pv(it - 1)
        for ev in list(pending_evac):
            if ev[3] == 0:
                emit_evac(ev[0], ev[1], 0)
                ev[3] = 1
            elif it >= ev[2] or it == n:
                emit_evac(ev[0], ev[1], 1)
                pending_evac.remove(ev)
    for ev in list(pending_evac):
        if ev[3] == 0:
            emit_evac(ev[0], ev[1], 0)
        emit_evac(ev[0], ev[1], 1)
    s.emit()
    return nc


def build_C():
    nc = _new_nc()
    xT = _din(nc, "xT", [D, TS], F32)
    attnT = _din(nc, "attnT", [512, TS], F32)
    sgun = _din(nc, "sgun", [512, TS], BF16)
    goa = _din(nc, "goa", [128, 4], F32)
    w_out = _din(nc, "w_out", [D, D], F32)
    gffn = _din(nc, "gffn", [128, 8], F32)
    w_gu = _din(nc, "w_gu", [D, 2 * DFF], F32)
    w_dn = _din(nc, "w_dn", [DFF, D], F32)
    gfin = _din(nc, "gfin", [128, 8], F32)
    xo = _dout(nc, "xo", [D, TS], F32)
    yo = _dout(nc, "yo", [D, TS], F32)

    X = _sb(nc, "X", [128, 8, TS], F32)
    WO = _sb(nc, "WO", [128, 8, D], BF16)
    GA = _sb(nc, "GA", [128, 4], F32)
    GF = _sb(nc, "GF", [128, 8], F32)
    GN = _sb(nc, "GN", [128, 8], F32)
    ONES = _sb(nc, "ONES", [128, 128], BF16)
    EPSC = _sb(nc, "EPSC", [128, 1], F32)
    AT = _sb(nc, "AT", [128, 4, 512], F32)
    SQ = _sb(nc, "SQ", [128, 8, 512], BF16)
    RSQ = _sb(nc, "RSQ", [128, 512], F32)
    RS = _sb(nc, "RS", [128, 512], F32)
    RSTD = _sb(nc, "RSTD", [128, 1024], F32)
    XB = _sb(nc, "XB", [128, 8, 1024], BF16)
    MG = XB[:, :, 0:512]
    ACTT = _sb(nc, "ACTT", [128, 22, 1024], BF16)
    NPAN = 3
    PAN = [_sb(nc, "PAN%d" % i, [128, 8, 2, 128], BF16) for i in range(NPAN)]
    NDPN = 2
    DPN = [_sb(nc, "DPN%d" % i, [128, 22, 128], BF16) for i in range(NDPN)]
    G1 = [_sb(nc, "G1_%d" % i, [128, 512], F32) for i in range(2)]
    SI = G1
    U1 = [_sb(nc, "U1_%d" % i, [128, 512], F32) for i in range(2)]
    YS = [_sb(nc, "YS%d" % i, [128, 8, 128], F32) for i in range(2)]

    P_SS = _ps(nc, "P_SS", [128, 512])
    P_A = [_ps(nc, "P_A%d" % i, [128, 512]) for i in range(2)]
    P_G = [_ps(nc, "P_G%d" % i, [128, 512]) for i in range(2)]
    P_U = [_ps(nc, "P_U%d" % i, [128, 512]) for i in range(2)]

    s = Sched(nc)
    s.dma("sp", GA[:, :], goa, writes=["GA"], slot="GA")
    s.dma("sp", GF[:, :], gffn, writes=["GF"], slot="GF")
    s.dma("sp", GN[:, :], gfin, writes=["GN"], slot="GN")
    xT_v = xT.rearrange("(c p) t -> p c t", p=128)
    for tg in range(4):
        s.dma("sp", X[:, :, tg * 512:(tg + 1) * 512], xT_v[:, :, tg * 512:(tg + 1) * 512],
              writes=["X%d" % tg], slot="X%d" % tg)
    wo_v = w_out.rearrange("(c p) n -> p c n", p=128)
    for h in range(2):
        s.dma("pool", WO[:, :, h * 512:(h + 1) * 512], wo_v[:, :, h * 512:(h + 1) * 512], writes=["WO%d" % h],
              slot="WO%d" % h)
    s.op("dve", lambda e: e.memset(ONES[:, :], 1.0), writes=["ONES"])
    s.op("dve", lambda e: e.memset(EPSC[:, :], EPS), writes=["EPSC"])

    at_v = attnT.rearrange("(c p) t -> p c t", p=128)
    sg_v = sgun.rearrange("(c p) t -> p c t", p=128)
    pa = [0]

    def rstd_from(P, n_feat, dst):
        s.op("act", lambda e: e.activation(out=RSQ[:, :], in_=P[:, :], func=AF.Sqrt, bias=EPSC[:, 0:1],
                                           scale=1.0 / n_feat), reads=["P_SS", "EPSC"], writes=["RSQ"])
        s.op("dve", lambda e: e.reciprocal(out=dst, in_=RSQ[:, :]), reads=["RSQ"], writes=["RSd"])

    for tg in range(4):
        t0 = tg * 512
        xr = "X%d" % tg
        s.dma("sp", AT[:, :, :], at_v[:, :, t0:t0 + 512], writes=["AT"], slot="AT")
        s.dma("sp", MG[:, 4:8, :], sg_v[:, :, t0:t0 + 512], writes=["MGs"], slot="MGs")
        s.op("act", lambda e: e.activation(out=SQ[:, 0:4, :], in_=AT[:, :, :], func=AF.Square),
             reads=["AT"], writes=["SQ"])
        for c in range(4):
            s.op("pe", lambda e, c=c: e.matmul(P_SS[:, :], lhsT=ONES[:, :], rhs=SQ[:, c, :], start=(c == 0),
                                               stop=(c == 3)), reads=["ONES", "SQ"], writes=["P_SS"])
        rstd_from(P_SS, 512, RS[:, :])
        for c in range(4):
            s.op("dve", lambda e, c=c: e.scalar_tensor_tensor(out=MG[:, c, :], in0=AT[:, c, :], scalar=GA[:, c:c + 1],
                                                            in1=RS[:, :], op0=ALU.mult, op1=ALU.mult),
                 reads=["AT", "GA", "RSd"], writes=["MGa"])
        for oc in range(8):
            pb = pa[0] % 2
            pa[0] += 1
            P = P_A[pb]
            for c in range(8):
                s.op("pe", lambda e, c=c, oc=oc, P=P: e.matmul(P[:, :], lhsT=WO[:, c, oc * 128:(oc + 1) * 128],
                                                             rhs=MG[:, c, :], start=(c == 0), stop=(c == 7)),
                     reads=["WO%d" % (oc // 4), "MGa", "MGs"], writes=["P_A%d" % pb])
            s.op("dve", lambda e, oc=oc, P=P, t0=t0: e.tensor_tensor(out=X[:, oc, t0:t0 + 512], in0=P[:, :],
                                                                   in1=X[:, oc, t0:t0 + 512], op=ALU.add),
                 reads=["P_A%d" % pb, xr], writes=[xr])

    wgu_v = w_gu.rearrange("(c p) n -> p c n", p=128)
    wdn_v = w_dn.rearrange("(j p) n -> p j n", p=128)
    pan_i = [0]
    dpn_i = [0]
    gi = [0]
    for hf in range(2):
        for grp in range(2):
            tg = hf * 2 + grp
            t0 = tg * 512
            xr = "X%d" % tg
            s.op("act", lambda e, t0=t0: e.activation(out=SQ[:, :, :], in_=X[:, :, t0:t0 + 512], func=AF.Square),
                 reads=[xr], writes=["SQ"])
            for c in range(8):
                s.op("pe", lambda e, c=c: e.matmul(P_SS[:, :], lhsT=ONES[:, :], rhs=SQ[:, c, :], start=(c == 0),
                                                   stop=(c == 7)), reads=["ONES", "SQ"], writes=["P_SS"])
            s.op("act", lambda e: e.activation(out=RSQ[:, :], in_=P_SS[:, :], func=AF.Sqrt, bias=EPSC[:, 0:1],
                                               scale=1.0 / D), reads=["P_SS", "EPSC"], writes=["RSQ"])
            s.op("dve", lambda e, grp=grp: e.reciprocal(out=RSTD[:, grp * 512:(grp + 1) * 512], in_=RSQ[:, :]),
                 reads=["RSQ"], writes=["RSTD%d" % grp])
            for c in range(8):
                eng = "dve" if c % 2 == 0 else "pool"
                s.op(eng, lambda e, c=c, t0=t0, grp=grp: e.tensor_scalar(
                    out=XB[:, c, grp * 512:(grp + 1) * 512], in0=X[:, c, t0:t0 + 512], scalar1=GF[:, c:c + 1],
                    scalar2=None, op0=ALU.mult), reads=[xr, "GF"],
                     writes=["XB%d" % grp] + (["MGa", "MGs"] if grp == 0 else []))
        for j in range(22):
            pi = pan_i[0] % NPAN
            pan_i[0] += 1
            s.dma("pool", PAN[pi][:, :, 0, :], wgu_v[:, :, j * 128:(j + 1) * 128], writes=["PAN%d" % pi],
                  slot="PANg%d" % pi)
            s.dma("pool", PAN[pi][:, :, 1, :], wgu_v[:, :, DFF + j * 128:DFF + (j + 1) * 128], writes=["PAN%d" % pi],
                  slot="PANu%d" % pi)
            for grp in range(2):
                gb = gi[0] % 2
                gi[0] += 1
                for c in range(8):
                    s.op("pe", lambda e, c=c, pi=pi, grp=grp, gb=gb: e.matmul(
                        P_G[gb][:, :], lhsT=PAN[pi][:, c, 0, :], rhs=XB[:, c, grp * 512:(grp + 1) * 512],
                        start=(c == 0), stop=(c == 7)), reads=["PAN%d" % pi, "XB%d" % grp], writes=["P_G%d" % gb])
                for c in range(8):
                    s.op("pe", lambda e, c=c, pi=pi, grp=grp, gb=gb: e.matmul(
                        P_U[gb][:, :], lhsT=PAN[pi][:, c, 1, :], rhs=XB[:, c, grp * 512:(grp + 1) * 512],
                        start=(c == 0), stop=(c == 7)), reads=["PAN%d" % pi, "XB%d" % grp], writes=["P_U%d" % gb])
                rs = RSTD[:, grp * 512:(grp + 1) * 512]
                s.op("dve", lambda e, gb=gb, rs=rs: e.tensor_tensor(out=G1[gb][:, :], in0=P_G[gb][:, :], in1=rs,
                                                                  op=ALU.mult),
                     reads=["P_G%d" % gb, "RSTD%d" % grp], writes=["G1_%d" % gb])
                s.op("act", lambda e, gb=gb: e.activation(out=SI[gb][:, :], in_=G1[gb][:, :], func=AF.Silu),
                     reads=["G1_%d" % gb], writes=["G1_%d" % gb])
                s.op("dve", lambda e, gb=gb, rs=rs: e.tensor_tensor(out=U1[gb][:, :], in0=P_U[gb][:, :], in1=rs,
                                                                  op=ALU.mult),
                     reads=["P_U%d" % gb, "RSTD%d" % grp], writes=["U1_%d" % gb])
                s.op("pool", lambda e, gb=gb, j=j, grp=grp: e.tensor_tensor(
                    out=ACTT[:, j, grp * 512:(grp + 1) * 512], in0=SI[gb][:, :], in1=U1[gb][:, :], op=ALU.mult),
                    reads=["G1_%d" % gb, "U1_%d" % gb], writes=["ACTT%d" % grp])
        for oc in range(8):
            di = dpn_i[0] % NDPN
            dpn_i[0] += 1
            s.dma("pool", DPN[di][:, :, :], wdn_v[:, :, oc * 128:(oc + 1) * 128], writes=["DPN%d" % di],
                  slot="DPN%d" % di)
            for grp in range(2):
                tg = hf * 2 + grp
                t0 = tg * 512
                xr = "X%d" % tg
                pb = pa[0] % 2
                pa[0] += 1
                P = P_A[pb]
                for j in range(22):
                    s.op("pe", lambda e, j=j, di=di, grp=grp, P=P: e.matmul(
                        P[:, :], lhsT=DPN[di][:, j, :], rhs=ACTT[:, j, grp * 512:(grp + 1) * 512],
                        start=(j == 0), stop=(j == 21)), reads=["DPN%d" % di, "ACTT%d" % grp], writes=["P_A%d" % pb])
                s.op("dve", lambda e, oc=oc, P=P, t0=t0: e.tensor_tensor(out=X[:, oc, t0:t0 + 512], in0=P[:, :],
                                                                       in1=X[:, oc, t0:t0 + 512], op=ALU.add),
                     reads=["P_A%d" % pb, xr], writes=[xr])

    xo_v = xo.rearrange("(c p) t -> p c t", p=128)
    yo_v = yo.rearrange("(c p) t -> p c t", p=128)
    for tg in range(4):
        t0 = tg * 512
        xr = "X%d" % tg
        s.dma("sp", xo_v[:, :, t0:t0 + 512], X[:, :, t0:t0 + 512], reads=[xr], writes=["o_x"], slot="XO%d" % tg)
        s.op("act", lambda e, t0=t0: e.activation(out=SQ[:, :, :], in_=X[:, :, t0:t0 + 512], func=AF.Square),
             reads=[xr], writes=["SQ"])
        for c in range(8):
            s.op("pe", lambda e, c=c: e.matmul(P_SS[:, :], lhsT=ONES[:, :], rhs=SQ[:, c, :], start=(c == 0),
                                               stop=(c == 7)), reads=["ONES", "SQ"], writes=["P_SS"])
        s.op("act", lambda e: e.activation(out=RSQ[:, :], in_=P_SS[:, :], func=AF.Sqrt, bias=EPSC[:, 0:1],
                                           scale=1.0 / D), reads=["P_SS", "EPSC"], writes=["RSQ"])
        s.op("dve", lambda e: e.reciprocal(out=RS[:, :], in_=RSQ[:, :]), reads=["RSQ"], writes=["RSd"])
        for hh in range(4):
            yb = hh % 2
            for c in range(8):
                s.op("dve", lambda e, c=c, t0=t0, hh=hh, yb=yb: e.scalar_tensor_tensor(
                    out=YS[yb][:, c, :], in0=X[:, c, t0 + hh * 128:t0 + (hh + 1) * 128], scalar=GN[:, c:c + 1],
                    in1=RS[:, hh * 128:(hh + 1) * 128], op0=ALU.mult, op1=ALU.mult),
                    reads=[xr, "GN", "RSd"], writes=["YS%d" % yb])
            s.dma("sp", yo_v[:, :, t0 + hh * 128:t0 + (hh + 1) * 128], YS[yb][:, :, :], reads=["YS%d" % yb],
                  writes=["o_y"], slot="YS%d" % yb)
    s.emit()
    return nc


_PROGS = {}


def _prog(name):
    if name not in _PROGS:
        _PROGS[name] = {"A": build_A, "B": build_B, "C": build_C}[name]()
    return _PROGS[name]


def _pc(vec, nchunk):
    return np.ascontiguousarray(np.asarray(vec, np.float32).reshape(nchunk, 128).T)


def _consts():
    identf = np.eye(128, dtype=np.float32)
    identb = identf.astype(ml_dtypes.bfloat16)
    tri = (np.arange(128)[:, None] < np.arange(128)[None, :]).astype(np.float32)
    p = np.arange(128)[:, None, None]
    kbd = np.arange(4)[None, :, None]
    j = np.arange(512)[None, None, :]
    maskneg = np.where(kbd * 128 + p > j, NEG, 0.0).astype(np.float32).astype(ml_dtypes.bfloat16)
    return identf, identb, tri, maskneg


def kernel(x, mix_norm_g, w_in, b_f, sgu_ln_g, sgu_ln_b, w_s, b_s, out_norm_g, w_out,
           ffn_norm_g, w_gate_up, w_down, final_norm_g):
    f32 = np.float32
    x = np.asarray(x, f32)
    cores = list(range(NCORES))
    identf, identb, tri, maskneg = _consts()
    xT = [np.ascontiguousarray(x[0, r * TS:(r + 1) * TS, :].T) for r in cores]
    y_shards = None
    for l in range(DEPTH):
        wl = np.ascontiguousarray(np.asarray(w_in[l], f32))
        lng = np.ascontiguousarray(np.broadcast_to(np.asarray(sgu_ln_g[l], f32)[None, :], (128, 512)))
        lnb = np.ascontiguousarray(np.broadcast_to(np.asarray(sgu_ln_b[l], f32)[None, :], (128, 512)))
        wsT = np.ascontiguousarray(np.transpose(np.asarray(w_s[l], f32), (2, 0, 1)))
        bsl = np.asarray(b_s[l], f32)
        bsb = np.ascontiguousarray(
            np.broadcast_to(bsl.reshape(4, 2, 1, 128), (4, 2, 64, 128)).transpose(1, 2, 0, 3).reshape(128, 4, 128))
        gmix = _pc(mix_norm_g[l], 8)
        gos = _pc(np.asarray(out_norm_g[l])[512:], 4)
        in_maps = [dict(xT=xT[r], gmix=gmix, w_in=wl, lng=lng, lnb=lnb, wsT=wsT, bsb=bsb, gos=gos, identf=identf)
                   for r in cores]
        resA = run_bass_kernel_spmd(_prog("A"), in_maps, core_ids=cores).results
        qkv = np.concatenate([np.asarray(resA[r]["qkvT"]) for r in cores], axis=1)
        fall = np.concatenate([np.asarray(resA[r]["fT"]) for r in cores], axis=1)
        sg = [np.asarray(resA[r]["sgun"]) for r in cores]
        in_maps = []
        for h in cores:
            in_maps.append(dict(
                qT=np.ascontiguousarray(qkv[h * 64:(h + 1) * 64]),
                kT=np.ascontiguousarray(qkv[512 + h * 64:512 + (h + 1) * 64]),
                v=np.ascontiguousarray(qkv[1024 + h * 64:1024 + (h + 1) * 64].T),
                f=np.ascontiguousarray(fall[h].reshape(128, 128)),
                bf=np.full((128, 1), np.asarray(b_f, f32)[l, h], f32),
                identf=identf, identb=identb, tri=tri, maskneg=maskneg))
        resB = run_bass_kernel_spmd(_prog("B"), in_maps, core_ids=cores).results
        attn = np.concatenate([np.asarray(resB[h]["oT"]) for h in cores], axis=0)
        goa = _pc(np.asarray(out_norm_g[l])[:512], 4)
        gffn = _pc(ffn_norm_g[l], 8)
        gfin = _pc(final_norm_g, 8)
        wo = np.ascontiguousarray(np.asarray(w_out[l], f32))
        wgu = np.ascontiguousarray(np.asarray(w_gate_up[l], f32))
        wdn = np.ascontiguousarray(np.asarray(w_down[l], f32))
        in_maps = [dict(xT=xT[r], attnT=np.ascontiguousarray(attn[:, r * TS:(r + 1) * TS]), sgun=sg[r], goa=goa,
                        w_out=wo, gffn=gffn, w_gu=wgu, w_dn=wdn, gfin=gfin) for r in cores]
        resC = run_bass_kernel_spmd(_prog("C"), in_maps, core_ids=cores).results
        xT = [np.asarray(resC[r]["xo"]) for r in cores]
        y_shards = [np.asarray(resC[r]["yo"]) for r in cores]
    out = np.concatenate([ys.T for ys in y_shards], axis=0)[None].astype(f32)
    return out
```

```python
import numpy as np
import ml_dtypes
import concourse.bass as bass
import concourse.mybir as mybir
from concourse.bass_utils import run_bass_kernel_spmd

F32 = mybir.dt.float32
BF16 = mybir.dt.bfloat16
AF = mybir.ActivationFunctionType
ALU = mybir.AluOpType

NCORES = 8
D = 1024
S = 16384
DEPTH = 4
TS = S // NCORES
DH = 64
NH = 8
D_IN = 2568
DFF = 2816
EPS = 1e-6
NEG = -30000.0


class Sched:
    ENG = ("pe", "act", "dve", "pool", "sp")

    def __init__(self, nc):
        self.nc = nc
        self.ops = []
        self.by_eng = {e: [] for e in self.ENG}
        self.last_w = {}
        self.readers = {}
        self.slot_count = {}

    def op(self, eng, fn, reads=(), writes=(), dma_slot=None):
        idx = len(self.ops)
        deps = set()
        for r in reads:
            if r in self.last_w:
                deps.add(self.last_w[r])
        for w in writes:
            if w in self.last_w:
                deps.add(self.last_w[w])
            for rd in self.readers.get(w, ()):
                deps.add(rd)
        deps.discard(idx)
        rec = dict(eng=eng, fn=fn, deps=deps, dma=dma_slot, signal=False, ordinal=None)
        if dma_slot is not None:
            self.slot_count[dma_slot] = self.slot_count.get(dma_slot, 0) + 1
            rec["ordinal"] = self.slot_count[dma_slot]
        self.ops.append(rec)
        self.by_eng[eng].append(idx)
        for r in reads:
            self.readers.setdefault(r, []).append(idx)
        for w in writes:
            self.last_w[w] = idx
            self.readers[w] = []
        return idx

    def dma(self, q, out, in_, reads=(), writes=(), slot=None, **kw):
        assert slot is not None
        return self.op(q, lambda e: e.dma_start(out=out, in_=in_, **kw), reads, writes, dma_slot=slot)

    def emit(self):
        nc = self.nc
        ops = self.ops
        for i, o in enumerate(ops):
            keep = set()
            for d in o["deps"]:
                od = ops[d]
                if od["dma"] is None and od["eng"] == o["eng"] and o["dma"] is None and o["eng"] == "pe":
                    continue
                keep.add(d)
                if od["dma"] is None:
                    od["signal"] = True
            o["deps"] = keep
        sigcount = {}
        for e in self.ENG:
            c = 0
            for i in self.by_eng[e]:
                if ops[i]["dma"] is None and ops[i]["signal"]:
                    c += 1
                    sigcount[i] = c
        import contextlib
        with contextlib.ExitStack() as st:
            esem = {e: st.enter_context(nc.semaphore("s_" + e)) for e in self.ENG}
            ssem = {s: st.enter_context(nc.semaphore("d_%d" % k)) for k, s in enumerate(self.slot_count)}
            block = st.enter_context(nc.Block())

            def run(ename, eng):
                waited = {}
                for i in self.by_eng[ename]:
                    o = ops[i]
                    need = {}
                    for d in o["deps"]:
                        od = ops[d]
                        if od["dma"] is not None:
                            key = ("d", od["dma"])
                            val = 16 * od["ordinal"]
                        else:
                            key = ("e", od["eng"])
                            val = sigcount[d]
                        if val > need.get(key, 0):
                            need[key] = val
                    for key, val in need.items():
                        if waited.get(key, 0) >= val:
                            continue
                        waited[key] = val
                        sem = ssem[key[1]] if key[0] == "d" else esem[key[1]]
                        eng.wait_ge(sem, val)
                    ins = o["fn"](eng)
                    if o["dma"] is not None:
                        ins.then_inc(ssem[o["dma"]], 16)
                    elif o["signal"]:
                        ins.then_inc(esem[ename], 1)
                if ename == "sp":
                    for s, n in self.slot_count.items():
                        eng.wait_ge(ssem[s], 16 * n)

            @block.tensor
            def _(e):
                run("pe", e)

            @block.scalar
            def _(e):
                run("act", e)

            @block.vector
            def _(e):
                run("dve", e)

            @block.gpsimd
            def _(e):
                run("pool", e)

            @block.sync
            def _(e):
                run("sp", e)


def _new_nc():
    return bass.Bass("TRN2", target_bir_lowering=False)


def _din(nc, name, shape, dt):
    return nc.dram_tensor(name, list(shape), dt, kind="ExternalInput").ap()


def _dout(nc, name, shape, dt):
    return nc.dram_tensor(name, list(shape), dt, kind="ExternalOutput").ap()


def _sb(nc, name, shape, dt):
    return nc.alloc_sbuf_tensor(name, list(shape), dt)


def _ps(nc, name, shape, dt=F32):
    return nc.alloc_psum_tensor(name, list(shape), dt)


class Sched2(Sched):
    def __init__(self, nc):
        super().__init__(nc)
        self.bar = set()
        self.bar_pending = set()
        self.last_eng = {}
        self.last_slot = {}
        self.inc = {}

    def op(self, eng, fn, reads=(), writes=(), dma_slot=None, inc=16):
        idx = super().op(eng, fn, reads, writes, dma_slot)
        if eng in self.bar_pending:
            self.ops[idx]["deps"] |= self.bar
            self.bar_pending.discard(eng)
        if dma_slot is None:
            self.last_eng[eng] = idx
        else:
            self.last_slot[dma_slot] = idx
            self.inc[dma_slot] = inc
        return idx

    def barrier(self):
        self.bar = set(self.last_eng.values()) | set(self.last_slot.values())
        self.bar_pending = set(self.ENG)

    def cc(self, in_t, out_t, reads, writes, slot):
        rg = [list(range(NCORES))]
        return self.op("pool", lambda e: e.collective_compute(
            "AllGather", ALU.bypass, replica_groups=rg, ins=[in_t.ap().opt()], outs=[out_t.ap().opt()]),
            reads, writes, dma_slot=slot, inc=1)

    def emit(self):
        nc = self.nc
        ops = self.ops
        for i, o in enumerate(ops):
            keep = set()
            for d in o["deps"]:
                od = ops[d]
                if od["dma"] is None and od["eng"] == o["eng"] and o["dma"] is None and o["eng"] == "pe":
                    continue
                keep.add(d)
                if od["dma"] is None:
                    od["signal"] = True
            o["deps"] = keep
        sigcount = {}
        for e in self.ENG:
            c = 0
            for i in self.by_eng[e]:
                if ops[i]["dma"] is None and ops[i]["signal"]:
                    c += 1
                    sigcount[i] = c
        import contextlib
        with contextlib.ExitStack() as st:
            esem = {e: st.enter_context(nc.semaphore("s_" + e)) for e in self.ENG}
            ssem = {s: st.enter_context(nc.semaphore("d_%d" % k)) for k, s in enumerate(self.slot_count)}
            block = st.enter_context(nc.Block())

            def run(ename, eng):
                waited = {}
                for i in self.by_eng[ename]:
                    o = ops[i]
                    need = {}
                    for d in o["deps"]:
                        od = ops[d]
                        if od["dma"] is not None:
                            key = ("d", od["dma"])
                            val = self.inc[od["dma"]] * od["ordinal"]
                        else:
                            key = ("e", od["eng"])
                            val = sigcount[d]
                        if val > need.get(key, 0):
                            need[key] = val
                    for key, val in need.items():
                        if waited.get(key, 0) >= val:
                            continue
                        waited[key] = val
                        sem = ssem[key[1]] if key[0] == "d" else esem[key[1]]
                        eng.wait_ge(sem, val)
                    ins = o["fn"](eng)
                    if o["dma"] is not None:
                        ins.then_inc(ssem[o["dma"]], self.inc[o["dma"]])
                    elif o["signal"]:
                        ins.then_inc(esem[ename], 1)
                if ename == "sp":
                    for s_, n in self.slot_count.items():
                        eng.wait_ge(ssem[s_], self.inc[s_] * n)

            @block.tensor
            def _(e):
                run("pe", e)

            @block.scalar
            def _(e):
                run("act", e)

            @block.vector
            def _(e):
                run("dve", e)

            @block.gpsimd
            def _(e):
                run("pool", e)

            @block.sync
            def _(e):
                run("sp", e)


_DTB = {F32: 4, BF16: 2}


class Plan:
    def __init__(self, nc):
        self.nc = nc
        self.lo = (nc.sbuf_base + 31) // 32 * 32
        self.hi = nc.sbuf_top
        self.ptr = self.lo
        self.n = 0

    def alloc(self, name, shape, dt):
        nbytes = _DTB[dt]
        for d in shape[1:]:
            nbytes *= d
        nbytes = (nbytes + 31) // 32 * 32
        off = self.ptr
        self.ptr += nbytes
        assert self.ptr <= self.hi, (name, self.ptr, self.hi)
        self.n += 1
        return self.nc.alloc_sbuf_tensor_at("%s_%d" % (name, self.n), list(shape), dt, offset=off)

    def mark(self):
        return self.ptr

    def reset(self, mark):
        self.ptr = mark


def build_fused(depth=DEPTH, stop='full'):
    nc = _new_nc()
    big = stop in ('full', 'C')
    xT = _din(nc, "xT", [D, TS], F32)
    gmix = _din(nc, "gmix", [128, depth, 8], F32)
    w_in = _din(nc, "w_in", [depth, D, D_IN], F32)
    lng = _din(nc, "lng", [depth, 128, 512], F32)
    lnb = _din(nc, "lnb", [depth, 128, 512], F32)
    wsT = _din(nc, "wsT", [depth, 128, 8, 128], F32)
    bsb = _din(nc, "bsb", [depth, 128, 4, 128], F32)
    gos = _din(nc, "gos", [128, depth, 4], F32)
    goa = _din(nc, "goa", [128, depth, 4], F32)
    gffn = _din(nc, "gffn", [128, depth, 8], F32)
    gfin = _din(nc, "gfin", [128, 8], F32)
    w_out = _din(nc, "w_out", [depth, D, D], F32) if big else None
    w_gu = _din(nc, "w_gu", [depth, D, 2 * DFF], F32) if big else None
    w_dn = _din(nc, "w_dn", [depth, DFF, D], F32) if big else None
    bfh = _din(nc, "bfh", [128, depth], F32)
    sel = _din(nc, "sel", [128, 4, 64], BF16)
    oh = _din(nc, "oh", [128, 8], F32)
    identf = _din(nc, "identf", [128, 128], F32)
    identb = _din(nc, "identb", [128, 128], BF16)
    tri = _din(nc, "tri", [128, 128], F32)
    maskneg = _din(nc, "maskneg", [128, 4, 512], BF16)
    yo = _dout(nc, "yo", [D, TS], F32)

    ag1_in = [nc.dram_tensor("ag1_in%d" % l, [1536, TS], BF16) for l in range(depth)]
    ag1_out = [nc.dram_tensor("ag1_out%d" % l, [NCORES * 1536, TS], BF16) for l in range(depth)]
    agf_in = [nc.dram_tensor("agf_in%d" % l, [8, TS], F32) for l in range(depth)]
    agf_out = [nc.dram_tensor("agf_out%d" % l, [NCORES * 8, TS], F32) for l in range(depth)]
    ag2_in = [nc.dram_tensor("ag2_in%d" % l, [512, TS], F32) for l in range(depth)]
    ag2_out = [nc.dram_tensor("ag2_out%d" % l, [NCORES * 512, TS], F32) for l in range(depth)]
    sg_scr = [nc.dram_tensor("sg_scr%d" % l, [512, TS], BF16) for l in range(depth)]
    crow = [nc.dram_tensor("crow%d" % l, [3, S], BF16) for l in range(depth)]

    pl = Plan(nc)
    X = pl.alloc("X", [128, 8, TS], F32)
    ONES = pl.alloc("ONES", [128, 128], BF16)
    ONEF = pl.alloc("ONEF", [128, 128], F32)
    IDF = pl.alloc("IDF", [128, 128], F32)
    IDB = pl.alloc("IDB", [128, 128], BF16)
    TRI = pl.alloc("TRI", [128, 128], F32)
    MN = pl.alloc("MN", [128, 4, 512], BF16)
    SEL = pl.alloc("SEL", [128, 4, 64], BF16)
    OH = pl.alloc("OH", [128, 8], F32)
    EPSC = pl.alloc("EPSC", [128, 1], F32)
    GMIX = pl.alloc("GMIX", [128, depth, 8], F32)
    GOS = pl.alloc("GOS", [128, depth, 4], F32)
    GOA = pl.alloc("GOA", [128, depth, 4], F32)
    GFFN = pl.alloc("GFFN", [128, depth, 8], F32)
    GFIN = pl.alloc("GFIN", [128, 8], F32)
    BFH = pl.alloc("BFH", [128, depth], F32)
    arena = pl.mark()

    PB = [nc.alloc_psum_tensor("PB%d" % i, [128, 1024], F32) for i in range(4)]

    s = Sched2(nc)
    for nm, t, src in [("IDF", IDF, identf), ("IDB", IDB, identb), ("TRI", TRI, tri), ("OH", OH, oh),
                       ("GFIN", GFIN, gfin), ("BFH", BFH, bfh)]:
        s.dma("sp", t[:, :], src, writes=[nm], slot=nm)
    for nm, t, src in [("MN", MN, maskneg), ("SEL", SEL, sel), ("GMIX", GMIX, gmix), ("GOS", GOS, gos),
                       ("GOA", GOA, goa), ("GFFN", GFFN, gffn)]:
        s.dma("sp", t[:, :, :], src, writes=[nm], slot=nm)
    xT_v = xT.rearrange("(c p) t -> p c t", p=128)
    for tg in range(4):
        s.dma("sp", X[:, :, tg * 512:(tg + 1) * 512], xT_v[:, :, tg * 512:(tg + 1) * 512],
              writes=["X%d" % tg], slot="X%d" % tg)
    s.op("dve", lambda e: e.memset(ONES[:, :], 1.0), writes=["ONES"])
    s.op("dve", lambda e: e.memset(EPSC[:, :], EPS), writes=["EPSC"])
    s.op("pool", lambda e: e.memset(ONEF[:, :], 1.0), writes=["ONEF"])

    pl.reset(arena)
    A = dict(
        W=pl.alloc("W", [128, 8, D_IN], BF16), LNG=pl.alloc("LNG", [128, 512], F32),
        LNB=pl.alloc("LNB", [128, 512], F32), WS=pl.alloc("WS", [128, 8, 128], BF16),
        BS=pl.alloc("BS", [128, 4, 128], F32), SQ=pl.alloc("SQ", [128, 8, 512], BF16),
        RSQ=pl.alloc("RSQ", [128, 512], F32), RSTD=pl.alloc("RSTD", [128, 512], F32),
        XB=pl.alloc("XB", [128, 8, 512], BF16),
        STG=[pl.alloc("STG%d" % i, [128, 512], BF16) for i in range(3)],
        FST=pl.alloc("FST", [8, 512], F32), ZR=[pl.alloc("ZR%d" % i, [128, 512], F32) for i in range(2)],
        ZU=pl.alloc("ZU", [128, 4, 512], F32), ZV=pl.alloc("ZV", [128, 4, 512], F32),
        ST6=pl.alloc("ST6", [128, 6], F32), MV=pl.alloc("MV", [128, 2], F32), SD=pl.alloc("SD", [128, 1], F32),
        RL=pl.alloc("RL", [128, 1], F32), ZN=pl.alloc("ZN", [128, 512], F32), ZN2=pl.alloc("ZN2", [128, 512], F32),
        ZNB=pl.alloc("ZNB", [128, 512], BF16), SG1=pl.alloc("SG1", [128, 4, 128], F32),
        SGU=pl.alloc("SGU", [128, 4, 512], F32), SQ2=pl.alloc("SQ2", [128, 4, 512], BF16),
        RSQ2=pl.alloc("RSQ2", [128, 512], F32), RS2=pl.alloc("RS2", [128, 512], F32),
        SGN=pl.alloc("SGN", [128, 4, 512], BF16))
    endA = pl.mark()
    pl.reset(arena)
    B = dict(
        QA=pl.alloc("QA", [67, S], BF16), KA=pl.alloc("KA", [67, S], BF16), V=pl.alloc("V", [128, 128, 66], BF16),
        PT=[pl.alloc("PT%d" % i, [128, 1024], BF16) for i in range(3)],
        LQ=[pl.alloc("LQ%d" % i, [128, 12, 256], BF16) for i in range(2)],
        FA=pl.alloc("FA", [128, 8, 128], F32),
        OSB=[pl.alloc("OSB%d" % i, [65, 512], F32) for i in range(2)],
        OTS=[pl.alloc("OTS%d" % i, [64, 512], F32) for i in range(2)],
        HI=pl.alloc("HI", [128, 3, 128], BF16))
    for nm in ["Fb", "Y", "A", "E", "L", "M", "LF", "SC", "C", "NEGC", "HF", "R1", "R2"]:
        B[nm] = pl.alloc(nm, [128, 128], F32)
    B["OFF"] = pl.alloc("OFF", [128, 2], F32)
    endB = pl.mark()
    pl.reset(arena)
    C = dict(
        WO=pl.alloc("WO", [128, 8, D], BF16), AT=pl.alloc("AT", [128, 4, 512], F32),
        SQ=pl.alloc("SQc", [128, 8, 512], BF16), RSQ=pl.alloc("RSQc", [128, 512], F32),
        RS=pl.alloc("RSc", [128, 512], F32), RSTD=pl.alloc("RSTDc", [128, 1024], F32),
        XB=pl.alloc("XBc", [128, 8, 1024], BF16),
        PAN=[pl.alloc("PAN%d" % i, [128, 8, 2, 128], BF16) for i in range(2)],
        DPN=[pl.alloc("DPN%d" % i, [128, 22, 128], BF16) for i in range(2)],
        G1=[pl.alloc("G1_%d" % i, [128, 512], F32) for i in range(2)],
        U1=[pl.alloc("U1_%d" % i, [128, 512], F32) for i in range(2)])
    actt_mark = pl.mark()
    C["ACTT"] = pl.alloc("ACTT", [128, 22, 1024], BF16)
    endC = pl.mark()
    pl.reset(actt_mark)
    C["TMP"] = [pl.alloc("TMP%d" % i, [128, 4, 512], F32) for i in range(5)]
    assert pl.mark() <= endC
    print("SBUF plan: arena", arena, "endA", endA, "endB", endB, "endC", endC, "hi", pl.hi)

    def phase_A(l):
        W, LNG, LNB, WS, BS, SQ, RSQ, RSTD, XB = (A[k] for k in ["W", "LNG", "LNB", "WS", "BS", "SQ", "RSQ", "RSTD", "XB"])
        STG, FST, ZR, ZU, ZV, ST6, MV, SD, RL = (A[k] for k in ["STG", "FST", "ZR", "ZU", "ZV", "ST6", "MV", "SD", "RL"])
        ZN, ZN2, ZNB, SG1, SGU, SQ2, RSQ2, RS2, SGN = (A[k] for k in ["ZN", "ZN2", "ZNB", "SG1", "SGU", "SQ2", "RSQ2", "RS2", "SGN"])
        P_SS = PB[0][:, 0:512]
        P_S2 = PB[0][:, 512:1024]
        P_PJ = [PB[1][:, 0:512], PB[1][:, 512:1024]]
        P_ZT = PB[2][:, 0:512]
        P_MX = PB[2][:, 512:1024].rearrange("p (a b) -> p a b", a=4)
        qkv_d = ag1_in[l].ap()
        f_d = agf_in[l].ap()
        sg_d = sg_scr[l].ap()
        s.dma("sp", LNG[:, :], lng[l], writes=["LNG"], slot="LNG")
        s.dma("sp", LNB[:, :], lnb[l], writes=["LNB"], slot="LNB")
        s.dma("sp", BS[:, :, :], bsb[l], writes=["BS"], slot="BS")
        w_v = w_in[l].rearrange("(c p) n -> p c n", p=128)
        panels = [(0, 512), (512, 1024), (1024, 1544), (1544, 2056), (2056, 2568)]
        for pi, (a, b) in enumerate(panels):
            s.dma("pool", W[:, :, a:b], w_v[:, :, a:b], writes=["W%d" % pi], slot="W%d" % pi)
        s.dma("pool", WS[:, :, :], wsT[l], writes=["WS"], slot="WS")
        s.op("pool", lambda e: e.memset(WS[64:128, :, 0:64], 0.0), reads=[], writes=["WS"])

        def wpanel(col):
            for pi, (a, b) in enumerate(panels):
                if a <= col < b:
                    return "W%d" % pi
            raise ValueError

        pj = [0]
        stg = [0]
        outs = []
        for tg in range(4):
            t0 = tg * 512
            xs = X[:, :, t0:t0 + 512]
            xr = "X%d" % tg
            s.op("act", lambda e, xs=xs: e.activation(out=SQ[:, :, :], in_=xs, func=AF.Square),
                 reads=[xr], writes=["SQ"])
            for c in range(8):
                s.op("pe", lambda e, c=c: e.matmul(P_SS, lhsT=ONES[:, :], rhs=SQ[:, c, :],
                                                   start=(c == 0), stop=(c == 7)),
                     reads=["ONES", "SQ"], writes=["P_SS"])
            s.op("act", lambda e: e.activation(out=RSQ[:, :], in_=P_SS, func=AF.Sqrt,
                                               bias=EPSC[:, 0:1], scale=1.0 / D),
                 reads=["P_SS", "EPSC"], writes=["RSQ"])
            s.op("dve", lambda e: e.reciprocal(out=RSTD[:, :], in_=RSQ[:, :]), reads=["RSQ"], writes=["RSTD"])
            for c in range(8):
                eng = "dve" if c % 2 == 0 else "pool"
                s.op(eng, lambda e, c=c, xs=xs: e.tensor_scalar(out=XB[:, c, :], in0=xs[:, c, :],
                                                               scalar1=GMIX[:, l, c:c + 1], scalar2=None,
                                                               op0=ALU.mult),
                     reads=[xr, "GMIX"], writes=["XB%d" % c])
            xbr = ["XB%d" % c for c in range(8)]
            for oc in range(12):
                pb = pj[0] % 2
                pj[0] += 1
                P = P_PJ[pb]
                for c in range(8):
                    s.op("pe", lambda e, c=c, oc=oc, P=P: e.matmul(P, lhsT=W[:, c, oc * 128:(oc + 1) * 128],
                                                                 rhs=XB[:, c, :], start=(c == 0), stop=(c == 7)),
                         reads=[wpanel(oc * 128)] + xbr, writes=["P_PJ%d" % pb])
                sb = stg[0] % 3
                stg[0] += 1
                sc = 0.125 if oc < 4 else 1.0
                s.op("dve", lambda e, P=P, sb=sb, sc=sc: e.scalar_tensor_tensor(
                    out=STG[sb][:, :], in0=P, scalar=sc, in1=RSTD[:, :], op0=ALU.mult, op1=ALU.mult),
                     reads=["P_PJ%d" % pb, "RSTD"], writes=["STG%d" % sb])
                on = "oq%d_%d_%d" % (l, tg, oc)
                outs.append(on)
                s.dma("sp", qkv_d[oc * 128:(oc + 1) * 128, t0:t0 + 512], STG[sb][:, :],
                      reads=["STG%d" % sb], writes=[on], slot="STG%d" % sb)
            pb = pj[0] % 2
            pj[0] += 1
            P = P_PJ[pb]
            for c in range(8):
                s.op("pe", lambda e, c=c, P=P: e.matmul(P[0:8, :], lhsT=W[:, c, 1536:1544], rhs=XB[:, c, :],
                                                       start=(c == 0), stop=(c == 7)),
                     reads=[wpanel(1536)] + xbr, writes=["P_PJ%d" % pb])
            s.op("dve", lambda e, P=P: e.tensor_tensor(out=FST[:, :], in0=P[0:8, :], in1=RSTD[0:8, :], op=ALU.mult),
                 reads=["P_PJ%d" % pb, "RSTD"], writes=["FST"])
            on = "of%d_%d" % (l, tg)
            outs.append(on)
            s.dma("sp", f_d[:, t0:t0 + 512], FST[:, :], reads=["FST"], writes=[on], slot="FST")
            for zc in range(8):
                pb = pj[0] % 2
                pj[0] += 1
                P = P_PJ[pb]
                col = 1544 + zc * 128
                for c in range(8):
                    s.op("pe", lambda e, c=c, col=col, P=P: e.matmul(P, lhsT=W[:, c, col:col + 128],
                                                                   rhs=XB[:, c, :], start=(c == 0), stop=(c == 7)),
                         reads=[wpanel(col), wpanel(col + 127)] + xbr, writes=["P_PJ%d" % pb])
                zb = zc % 2
                s.op("dve", lambda e, P=P, zb=zb: e.tensor_tensor(out=ZR[zb][:, :], in0=P, in1=RSTD[:, :],
                                                                op=ALU.mult),
                     reads=["P_PJ%d" % pb, "RSTD"], writes=["ZR%d" % zb])
                dst = ZU[:, zc, :] if zc < 4 else ZV[:, zc - 4, :]
                dr = ("ZU%d" % zc) if zc < 4 else ("ZV%d" % (zc - 4))
                s.op("act", lambda e, zb=zb, dst=dst: e.activation(out=dst, in_=ZR[zb][:, :], func=AF.Gelu),
                     reads=["ZR%d" % zb], writes=[dr])
            zur = ["ZU%d" % c for c in range(4)]
            for tt in range(4):
                a0 = tt * 128
                for c4 in range(4):
                    s.op("pe", lambda e, c4=c4, a0=a0: e.transpose(out=P_ZT[:, c4 * 128:(c4 + 1) * 128],
                                                                 in_=ZV[:, c4, a0:a0 + 128], identity=IDF[:, :]),
                         reads=["ZV%d" % c4, "IDF"], writes=["P_ZT"])
                s.op("dve", lambda e: e.bn_stats(out=ST6[:, :], in_=P_ZT), reads=["P_ZT"], writes=["ST6"])
                s.op("dve", lambda e: e.bn_aggr(out=MV[:, :], in_=ST6[:, :]), reads=["ST6"], writes=["MV"])
                s.op("act", lambda e: e.activation(out=SD[:, :], in_=MV[:, 1:2], func=AF.Sqrt, bias=EPSC[:, 0:1],
                                                   scale=1.0),
                     reads=["MV", "EPSC"], writes=["SD"])
                s.op("dve", lambda e: e.reciprocal(out=RL[:, :], in_=SD[:, :]), reads=["SD"], writes=["RL"])
                s.op("dve", lambda e: e.tensor_scalar(out=ZN[:, :], in0=P_ZT, scalar1=MV[:, 0:1],
                                                      scalar2=RL[:, 0:1], op0=ALU.subtract, op1=ALU.mult),
                     reads=["P_ZT", "MV", "RL"], writes=["ZN"])
                s.op("pool", lambda e: e.tensor_tensor(out=ZN2[:, :], in0=ZN[:, :], in1=LNG[:, :], op=ALU.mult),
                     reads=["ZN", "LNG"], writes=["ZN2"])
                s.op("dve", lambda e: e.tensor_tensor(out=ZNB[:, :], in0=ZN2[:, :], in1=LNB[:, :], op=ALU.add),
                     reads=["ZN2", "LNB"], writes=["ZNB"])
                for g in range(8):
                    po = (g % 2) * 64
                    s.op("pe", lambda e, g=g, po=po: e.matmul(P_MX[po:po + 64, g // 2, :],
                                                            lhsT=ZNB[:, g * 64:(g + 1) * 64], rhs=WS[:, g, :],
                                                            start=True, stop=True),
                         reads=["ZNB", "WS"], writes=["P_MX"])
                s.op("dve", lambda e: e.tensor_tensor(out=SG1[:, :, :], in0=P_MX, in1=BS[:, :, :], op=ALU.add),
                     reads=["P_MX", "BS"], writes=["SG1"])
                s.op("pool", lambda e, a0=a0: e.tensor_tensor(out=SGU[:, :, a0:a0 + 128], in0=SG1[:, :, :],
                                                            in1=ZU[:, :, a0:a0 + 128], op=ALU.mult),
                     reads=["SG1"] + zur, writes=["SGU"])
            s.op("act", lambda e: e.activation(out=SQ2[:, :, :], in_=SGU[:, :, :], func=AF.Square),
                 reads=["SGU"], writes=["SQ2"])
            for c in range(4):
                s.op("pe", lambda e, c=c: e.matmul(P_S2, lhsT=ONES[:, :], rhs=SQ2[:, c, :],
                                                   start=(c == 0), stop=(c == 3)),
                     reads=["ONES", "SQ2"], writes=["P_S2"])
            s.op("act", lambda e: e.activation(out=RSQ2[:, :], in_=P_S2, func=AF.Sqrt, bias=EPSC[:, 0:1],
                                               scale=1.0 / 512),
                 reads=["P_S2", "EPSC"], writes=["RSQ2"])
            s.op("dve", lambda e: e.reciprocal(out=RS2[:, :], in_=RSQ2[:, :]), reads=["RSQ2"], writes=["RS2"])
            for c in range(4):
                s.op("dve", lambda e, c=c: e.scalar_tensor_tensor(out=SGN[:, c, :], in0=SGU[:, c, :],
                                                                scalar=GOS[:, l, c:c + 1], in1=RS2[:, :],
                                                                op0=ALU.mult, op1=ALU.mult),
                     reads=["SGU", "GOS", "RS2"], writes=["SGN"])
            s.dma("sp", sg_d.rearrange("(c p) t -> p c t", p=128)[:, :, t0:t0 + 512], SGN[:, :, :],
                  reads=["SGN"], writes=["sg_scr"], slot="SGN")
        return outs

    def phase_B(l, upto=9):
        QA, KA, V, PT, LQ, FA, OSB, OTS, HI = (B[k] for k in ["QA", "KA", "V", "PT", "LQ", "FA", "OSB", "OTS", "HI"])
        Fb, Y, A_, E, L, M, LF, SC, C_, NEGC, HF, R1, R2, OFF = (B[k] for k in
            ["Fb", "Y", "A", "E", "L", "M", "LF", "SC", "C", "NEGC", "HF", "R1", "R2", "OFF"])
        PS_S = [PB[0], PB[1]]
        PS_O = [[PB[2][:, 0:512], PB[2][:, 512:1024]], [PB[3][:, 0:512], PB[3][:, 512:1024]]]
        a1 = ag1_out[l].ap()
        af = agf_out[l].ap()
        o_d = ag2_in[l].ap()
        crow_ap = crow[l].ap()
        s.op("pool", lambda e: e.memset(V[:, :, 64:65], 1.0), writes=["Vo"])
        s.op("pool", lambda e: e.memset(KA[64:67, :], 1.0), writes=["KAo"])
        for r in range(NCORES):
            s.dma("sp", FA[r * 16:(r + 1) * 16, :, :], af[r * 8:(r + 1) * 8, :].rearrange("h (g j) -> g h j", j=128),
                  writes=["FA%d" % r], slot="FA%d" % r)
        far = ["FA%d" % r for r in range(NCORES)]
        s.op("dve", lambda e: e.tensor_scalar(out=Fb[:, :], in0=FA[:, 0, :], scalar1=OH[:, 0:1], scalar2=None,
                                              op0=ALU.mult), reads=far + ["OH"], writes=["Fb"])
        for h in range(1, 8):
            s.op("dve", lambda e, h=h: e.scalar_tensor_tensor(out=Fb[:, :], in0=FA[:, h, :], scalar=OH[:, h:h + 1],
                                                            in1=Fb[:, :], op0=ALU.mult, op1=ALU.add),
                 reads=far + ["OH", "Fb"], writes=["Fb"])
        s.op("dve", lambda e: e.tensor_scalar(out=Y[:, :], in0=Fb[:, :], scalar1=BFH[:, l:l + 1], scalar2=-1.0,
                                              op0=ALU.add, op1=ALU.mult), reads=["Fb", "BFH"], writes=["Y"])
        s.op("dve", lambda e: e.tensor_scalar(out=E[:, :], in0=Y[:, :], scalar1=-1.0, scalar2=None, op0=ALU.mult),
             reads=["Y"], writes=["E"])
        s.op("dve", lambda e: e.tensor_tensor(out=A_[:, :], in0=Y[:, :], in1=E[:, :], op=ALU.max),
             reads=["Y", "E"], writes=["A"])
        s.op("act", lambda e: e.activation(out=E[:, :], in_=A_[:, :], func=AF.Exp, scale=-1.0),
             reads=["A"], writes=["E"])
        s.op("act", lambda e: e.activation(out=L[:, :], in_=E[:, :], func=AF.Ln, bias=ONEF[:, 0:1], scale=1.0),
             reads=["E", "ONEF"], writes=["L"])
        s.op("dve", lambda e: e.tensor_single_scalar(out=M[:, :], in_=Y[:, :], scalar=0.0, op=ALU.max),
             reads=["Y"], writes=["M"])
        s.op("dve", lambda e: e.scalar_tensor_tensor(out=LF[:, :], in0=M[:, :], scalar=-1.0, in1=L[:, :],
                                                     op0=ALU.mult, op1=ALU.subtract), reads=["M", "L"], writes=["LF"])
        s.op("dve", lambda e: e.tensor_tensor_scan(out=SC[:, :], data0=ONEF[:, :], data1=LF[:, :], initial=0.0,
                                                   op0=ALU.mult, op1=ALU.add), reads=["ONEF", "LF"], writes=["SC"])
        PO = PS_O[0][0]
        s.op("pe", lambda e: e.matmul(PO[:, 0:2], lhsT=TRI[:, :], rhs=SC[:, 126:128], start=True, stop=True),
             reads=["TRI", "SC"], writes=["PS_O00"])
        s.op("dve", lambda e: e.tensor_copy(out=OFF[:, :], in_=PO[:, 0:2]), reads=["PS_O00"], writes=["OFF"])
        s.op("dve", lambda e: e.tensor_scalar(out=C_[:, :], in0=SC[:, :], scalar1=OFF[:, 1:2], scalar2=None,
                                              op0=ALU.add), reads=["SC", "OFF"], writes=["C"])
        PO2 = PS_O[0][1]
        s.op("pe", lambda e: e.transpose(out=PO2[:, 0:128], in_=C_[:, :], identity=IDF[:, :]),
             reads=["C", "IDF"], writes=["PS_O01"])
        s.op("dve", lambda e: e.tensor_scalar(out=NEGC[:, :], in0=PO2[:, 0:128], scalar1=-1.0, scalar2=None,
                                              op0=ALU.mult), reads=["PS_O01"], writes=["NEGC"])
        s.op("dve", lambda e: e.tensor_copy(out=HI[:, 0, :], in_=C_[:, :]), reads=["C"], writes=["HI0"])
        s.op("dve", lambda e: e.tensor_copy(out=HF[:, :], in_=HI[:, 0, :]), reads=["HI0"], writes=["HF"])
        s.op("dve", lambda e: e.tensor_tensor(out=R1[:, :], in0=C_[:, :], in1=HF[:, :], op=ALU.subtract),
             reads=["C", "HF"], writes=["R1"])
        s.op("dve", lambda e: e.tensor_copy(out=HI[:, 1, :], in_=R1[:, :]), reads=["R1"], writes=["HI1"])
        s.op("dve", lambda e: e.tensor_copy(out=HF[:, :], in_=HI[:, 1, :]), reads=["HI1"], writes=["HF"])
        s.op("dve", lambda e: e.tensor_tensor(out=R2[:, :], in0=R1[:, :], in1=HF[:, :], op=ALU.subtract),
             reads=["R1", "HF"], writes=["R2"])
        s.op("dve", lambda e: e.tensor_copy(out=HI[:, 2, :], in_=R2[:, :]), reads=["R2"], writes=["HI2"])
        for r in range(3):
            s.dma("sp", crow_ap[r:r + 1, :].rearrange("o (p j) -> (o p) j", p=128), HI[:, r, :],
                  reads=["HI%d" % r], writes=["crow%d" % r], slot="crow%d" % r)
        s.dma("sp", QA[64:67, :], crow_ap[:, :], reads=["crow0", "crow1", "crow2"], writes=["QAc"], slot="QAc")

        if upto == 0:
            return []
        for hg in range(64):
            r = hg // 8
            col0 = (hg % 8) * 256
            tok0 = hg * 256
            lb = hg % 2
            s.dma("sp", LQ[lb][:, :, :],
                  a1[r * 1536:(r + 1) * 1536, col0:col0 + 256].rearrange("(c p) t -> p c t", p=128),
                  writes=["LQ%d" % lb], slot="LQ%d" % lb)
            PSq = PS_S[lb][0:64, 0:256]
            PSk = PS_S[lb][0:64, 512:768]
            PSv = PB[3][:, lb * 512:lb * 512 + 128].rearrange("p (a b) -> p a b", a=2)
            for c in range(4):
                s.op("pe", lambda e, c=c, lb=lb, PSq=PSq: e.matmul(PSq, lhsT=SEL[:, c, :], rhs=LQ[lb][:, c, :],
                                                                 start=(c == 0), stop=(c == 3)),
                     reads=["SEL", "LQ%d" % lb], writes=["psq%d" % lb])
            for c in range(4):
                s.op("pe", lambda e, c=c, lb=lb, PSk=PSk: e.matmul(PSk, lhsT=SEL[:, c, :], rhs=LQ[lb][:, 4 + c, :],
                                                                 start=(c == 0), stop=(c == 3)),
                     reads=["SEL", "LQ%d" % lb], writes=["psk%d" % lb])
            s.op("act", lambda e, lb=lb, PSq=PSq, tok0=tok0: e.activation(out=QA[0:64, tok0:tok0 + 256], in_=PSq,
                                                                        func=AF.Copy),
                 reads=["psq%d" % lb], writes=["QAq"])
            s.op("dve", lambda e, lb=lb, PSk=PSk, tok0=tok0: e.tensor_copy(out=KA[0:64, tok0:tok0 + 256], in_=PSk),
                 reads=["psk%d" % lb], writes=["KAk"])
            for tt in range(2):
                for c in range(4):
                    s.op("pe", lambda e, c=c, lb=lb, tt=tt, PSv=PSv: e.matmul(
                        PSv[:, tt, :], lhsT=LQ[lb][:, 8 + c, tt * 128:(tt + 1) * 128], rhs=SEL[:, c, :],
                        start=(c == 0), stop=(c == 3)), reads=["SEL", "LQ%d" % lb], writes=["psv%d" % lb])
            blk = hg * 2
            s.op("dve", lambda e, lb=lb, PSv=PSv, blk=blk: e.tensor_copy(out=V[:, blk:blk + 2, 0:64], in_=PSv),
                 reads=["psv%d" % lb], writes=["Vv"])

        if upto == 1:
            return []
        items = []
        for P in range(16):
            for kb in range(8 * P + 8):
                items.append((P, kb))
        n = len(items)
        pending_evac = []
        first = [True]

        def emit_qk(it):
            P, kb = items[it]
            sb = it % 2
            SS = PS_S[sb]
            subs = [0, 1] if kb <= 8 * P + 3 else [1]
            extra = ["psq0", "psq1", "psk0", "psk1", "psv0", "psv1"] if it < 2 else []
            for sub in subs:
                q0 = P * 1024 + sub * 512
                kbd = kb - (8 * P + 4 * sub)
                diag = 0 <= kbd <= 3
                s.op("pe", lambda e, SS=SS, sub=sub, q0=q0, kb=kb, diag=diag: e.matmul(
                    SS[:, sub * 512:(sub + 1) * 512], lhsT=KA[0:67, kb * 128:(kb + 1) * 128],
                    rhs=QA[0:67, q0:q0 + 512], start=True, stop=(not diag)),
                    reads=["KAk", "KAo", "QAq", "QAc"], writes=["PS_S%d" % sb] + extra)
                if diag:
                    s.op("pe", lambda e, SS=SS, sub=sub, kbd=kbd: e.matmul(
                        SS[:, sub * 512:(sub + 1) * 512], lhsT=IDB[:, :], rhs=MN[:, kbd, :], start=False, stop=True),
                        reads=["IDB", "MN"], writes=["PS_S%d" % sb])
            lo = subs[0] * 512
            pb = it % 3
            s.op("act", lambda e, SS=SS, lo=lo, pb=pb, kb=kb: e.activation(
                out=PT[pb][:, lo:1024], in_=SS[:, lo:1024], func=AF.Exp, bias=NEGC[:, kb:kb + 1], scale=1.0),
                reads=["PS_S%d" % sb, "NEGC"], writes=["PT%d" % pb])

        def emit_pv(it):
            P, kb = items[it]
            ob = P % 2
            pb = it % 3
            subs = [0, 1] if kb <= 8 * P + 3 else [1]
            for sub in subs:
                last = 8 * P + 4 * sub + 3
                PO_ = PS_O[ob][sub]
                s.op("pe", lambda e, PO_=PO_, kb=kb, pb=pb, sub=sub, last=last: e.matmul(
                    PO_[0:65, :], lhsT=V[:, kb, 0:65], rhs=PT[pb][:, sub * 512:(sub + 1) * 512],
                    start=(kb == 0), stop=(kb == last)), reads=["Vv", "Vo", "PT%d" % pb],
                    writes=["PS_O%d%d" % (ob, sub)])
                if kb == last:
                    pending_evac.append([P, sub, it + 3, 0])

        o_names = []

        def emit_evac(P, sub, stage):
            ob = P % 2
            PO_ = PS_O[ob][sub]
            por = "PS_O%d%d" % (ob, sub)
            q0 = P * 1024 + sub * 512
            if stage == 0:
                s.op("dve", lambda e: e.tensor_copy(out=OSB[sub][:, :], in_=PO_[0:65, :]), reads=[por],
                     writes=["OSB%d" % sub])
                s.op("dve", lambda e: e.reciprocal(out=OSB[sub][64:65, :], in_=OSB[sub][64:65, :]),
                     reads=["OSB%d" % sub], writes=["OSB%d" % sub])
            else:
                s.op("pe", lambda e: e.matmul(PO_[0:64, :], lhsT=ONEF[64:65, 0:64], rhs=OSB[sub][64:65, :],
                                              start=True, stop=True), reads=["ONEF", "OSB%d" % sub], writes=[por])
                s.op("dve", lambda e: e.tensor_tensor(out=OTS[sub][:, :], in0=OSB[sub][0:64, :], in1=PO_[0:64, :],
                                                      op=ALU.mult), reads=["OSB%d" % sub, por], writes=["OTS%d" % sub])
                sh = q0 // TS
                c0 = q0 % TS
                on = "oo%d_%d_%d" % (l, P, sub)
                o_names.append(on)
                s.dma("sp", o_d[sh * 64:(sh + 1) * 64, c0:c0 + 512], OTS[sub][:, :], reads=["OTS%d" % sub],
                      writes=[on], slot="OTS%d" % sub)

        for it in range(n + 1):
            if it < n:
                emit_qk(it)
            if it >= 1:
                emit_pv(it - 1)
            for ev in list(pending_evac):
                if ev[3] == 0:
                    emit_evac(ev[0], ev[1], 0)
                    ev[3] = 1
                elif it >= ev[2] or it == n:
                    emit_evac(ev[0], ev[1], 1)
                    pending_evac.remove(ev)
        for ev in list(pending_evac):
            if ev[3] == 0:
                emit_evac(ev[0], ev[1], 0)
            emit_evac(ev[0], ev[1], 1)
        return o_names

    def phase_C(l, last):
        WO, AT, SQ, RSQ, RS, RSTD, XB, PAN, DPN, G1, U1, ACTT, TMP = (C[k] for k in
            ["WO", "AT", "SQ", "RSQ", "RS", "RSTD", "XB", "PAN", "DPN", "G1", "U1", "ACTT", "TMP"])
        MG = XB[:, :, 0:512]
        P_SS = PB[0][:, 0:512]
        P_A = [PB[1][:, 0:512], PB[1][:, 512:1024]]
        P_G = [PB[2][:, 0:512], PB[2][:, 512:1024]]
        P_U = [PB[3][:, 0:512], PB[3][:, 512:1024]]
        a2 = ag2_out[l].ap().rearrange("(c h2 s d) t -> c h2 s d t", c=4, h2=2, s=8, d=64)
        sg_v = sg_scr[l].ap().rearrange("(c p) t -> p c t", p=128)
        wo_v = w_out[l].rearrange("(c p) n -> p c n", p=128)
        for h in range(2):
            s.dma("pool", WO[:, :, h * 512:(h + 1) * 512], wo_v[:, :, h * 512:(h + 1) * 512], writes=["WO%d" % h],
                  slot="WO%d" % h)
        pa = [0]
        tmpi = [0]
        for tg in range(4):
            t0 = tg * 512
            xr = "X%d" % tg
            s.dma("sp", MG[:, 4:8, :], sg_v[:, :, t0:t0 + 512], writes=["MGs"], slot="MGs")
            for sh in range(8):
                tb = tmpi[0] % 5
                tmpi[0] += 1
                for h2 in range(2):
                    s.dma("sp", TMP[tb][h2 * 64:(h2 + 1) * 64, :, :],
                          a2[:, h2, sh, :, t0:t0 + 512].rearrange("c d t -> d c t"),
                          writes=["TMP%d_%d" % (tb, h2)], slot="TMP%d_%d" % (tb, h2))
                tr = ["TMP%d_0" % tb, "TMP%d_1" % tb]
                if sh == 0:
                    s.op("dve", lambda e, tb=tb: e.tensor_scalar(out=AT[:, :, :], in0=TMP[tb][:, :, :],
                                                               scalar1=OH[:, 0:1], scalar2=None, op0=ALU.mult),
                         reads=tr + ["OH"], writes=["AT"])
                else:
                    s.op("dve", lambda e, tb=tb, sh=sh: e.scalar_tensor_tensor(
                        out=AT[:, :, :], in0=TMP[tb][:, :, :], scalar=OH[:, sh:sh + 1], in1=AT[:, :, :],
                        op0=ALU.mult, op1=ALU.add), reads=tr + ["OH", "AT"], writes=["AT"])
            s.op("act", lambda e: e.activation(out=SQ[:, 0:4, :], in_=AT[:, :, :], func=AF.Square),
                 reads=["AT"], writes=["SQ"])
            for c in range(4):
                s.op("pe", lambda e, c=c: e.matmul(P_SS, lhsT=ONES[:, :], rhs=SQ[:, c, :], start=(c == 0),
                                                   stop=(c == 3)), reads=["ONES", "SQ"], writes=["P_SS"])
            s.op("act", lambda e: e.activation(out=RSQ[:, :], in_=P_SS, func=AF.Sqrt, bias=EPSC[:, 0:1],
                                               scale=1.0 / 512), reads=["P_SS", "EPSC"], writes=["RSQ"])
            s.op("dve", lambda e: e.reciprocal(out=RS[:, :], in_=RSQ[:, :]), reads=["RSQ"], writes=["RSd"])
            for c in range(4):
                s.op("dve", lambda e, c=c: e.scalar_tensor_tensor(out=MG[:, c, :], in0=AT[:, c, :],
                                                                scalar=GOA[:, l, c:c + 1], in1=RS[:, :],
                                                                op0=ALU.mult, op1=ALU.mult),
                     reads=["AT", "GOA", "RSd"], writes=["MGa"])
            for oc in range(8):
                pb = pa[0] % 2
                pa[0] += 1
                P = P_A[pb]
                for c in range(8):
                    s.op("pe", lambda e, c=c, oc=oc, P=P: e.matmul(P, lhsT=WO[:, c, oc * 128:(oc + 1) * 128],
                                                                 rhs=MG[:, c, :], start=(c == 0), stop=(c == 7)),
                         reads=["WO%d" % (oc // 4), "MGa", "MGs"], writes=["P_A%d" % pb])
                s.op("dve", lambda e, oc=oc, P=P, t0=t0: e.tensor_tensor(out=X[:, oc, t0:t0 + 512], in0=P,
                                                                       in1=X[:, oc, t0:t0 + 512], op=ALU.add),
                     reads=["P_A%d" % pb, xr], writes=[xr])
        wgu_v = w_gu[l].rearrange("(c p) n -> p c n", p=128)
        wdn_v = w_dn[l].rearrange("(j p) n -> p j n", p=128)
        pan_i = [0]
        dpn_i = [0]
        gi = [0]
        for hf in range(2):
            for grp in range(2):
                tg = hf * 2 + grp
                t0 = tg * 512
                xr = "X%d" % tg
                s.op("act", lambda e, t0=t0: e.activation(out=SQ[:, :, :], in_=X[:, :, t0:t0 + 512], func=AF.Square),
                     reads=[xr], writes=["SQ"])
                for c in range(8):
                    s.op("pe", lambda e, c=c: e.matmul(P_SS, lhsT=ONES[:, :], rhs=SQ[:, c, :], start=(c == 0),
                                                       stop=(c == 7)), reads=["ONES", "SQ"], writes=["P_SS"])
                s.op("act", lambda e: e.activation(out=RSQ[:, :], in_=P_SS, func=AF.Sqrt, bias=EPSC[:, 0:1],
                                                   scale=1.0 / D), reads=["P_SS", "EPSC"], writes=["RSQ"])
                s.op("dve", lambda e, grp=grp: e.reciprocal(out=RSTD[:, grp * 512:(grp + 1) * 512], in_=RSQ[:, :]),
                     reads=["RSQ"], writes=["RSTD%d" % grp])
                for c in range(8):
                    eng = "dve" if c % 2 == 0 else "pool"
                    s.op(eng, lambda e, c=c, t0=t0, grp=grp: e.tensor_scalar(
                        out=XB[:, c, grp * 512:(grp + 1) * 512], in0=X[:, c, t0:t0 + 512],
                        scalar1=GFFN[:, l, c:c + 1], scalar2=None, op0=ALU.mult), reads=[xr, "GFFN"],
                         writes=["XB%d" % grp] + (["MGa", "MGs"] if grp == 0 else []))
            for j in range(22):
                pi = pan_i[0] % 2
                pan_i[0] += 1
                s.dma("pool", PAN[pi][:, :, 0, :], wgu_v[:, :, j * 128:(j + 1) * 128], writes=["PANg%d" % pi],
                      slot="PANg%d" % pi)
                s.dma("pool", PAN[pi][:, :, 1, :], wgu_v[:, :, DFF + j * 128:DFF + (j + 1) * 128],
                      writes=["PANu%d" % pi], slot="PANu%d" % pi)
                for grp in range(2):
                    gb = gi[0] % 2
                    gi[0] += 1
                    for c in range(8):
                        s.op("pe", lambda e, c=c, pi=pi, grp=grp, gb=gb: e.matmul(
                            P_G[gb], lhsT=PAN[pi][:, c, 0, :], rhs=XB[:, c, grp * 512:(grp + 1) * 512],
                            start=(c == 0), stop=(c == 7)), reads=["PANg%d" % pi, "XB%d" % grp],
                            writes=["P_G%d" % gb])
                    for c in range(8):
                        s.op("pe", lambda e, c=c, pi=pi, grp=grp, gb=gb: e.matmul(
                            P_U[gb], lhsT=PAN[pi][:, c, 1, :], rhs=XB[:, c, grp * 512:(grp + 1) * 512],
                            start=(c == 0), stop=(c == 7)), reads=["PANu%d" % pi, "XB%d" % grp],
                            writes=["P_U%d" % gb])
                    rs = RSTD[:, grp * 512:(grp + 1) * 512]
                    s.op("dve", lambda e, gb=gb, rs=rs: e.tensor_tensor(out=G1[gb][:, :], in0=P_G[gb], in1=rs,
                                                                      op=ALU.mult),
                         reads=["P_G%d" % gb, "RSTD%d" % grp], writes=["G1_%d" % gb])
                    s.op("act", lambda e, gb=gb: e.activation(out=G1[gb][:, :], in_=G1[gb][:, :], func=AF.Silu),
                         reads=["G1_%d" % gb], writes=["G1_%d" % gb])
                    s.op("dve", lambda e, gb=gb, rs=rs: e.tensor_tensor(out=U1[gb][:, :], in0=P_U[gb], in1=rs,
                                                                      op=ALU.mult),
                         reads=["P_U%d" % gb, "RSTD%d" % grp], writes=["U1_%d" % gb])
                    s.op("pool", lambda e, gb=gb, j=j, grp=grp: e.tensor_tensor(
                        out=ACTT[:, j, grp * 512:(grp + 1) * 512], in0=G1[gb][:, :], in1=U1[gb][:, :], op=ALU.mult),
                        reads=["G1_%d" % gb, "U1_%d" % gb], writes=["ACTT%d" % grp])
            for oc in range(8):
                di = dpn_i[0] % 2
                dpn_i[0] += 1
                s.dma("pool", DPN[di][:, :, :], wdn_v[:, :, oc * 128:(oc + 1) * 128], writes=["DPN%d" % di],
                      slot="DPN%d" % di)
                for grp in range(2):
                    tg = hf * 2 + grp
                    t0 = tg * 512
                    xr = "X%d" % tg
                    pb = pa[0] % 2
                    pa[0] += 1
                    P = P_A[pb]
                    for j in range(22):
                        s.op("pe", lambda e, j=j, di=di, grp=grp, P=P: e.matmul(
                            P, lhsT=DPN[di][:, j, :], rhs=ACTT[:, j, grp * 512:(grp + 1) * 512],
                            start=(j == 0), stop=(j == 21)), reads=["DPN%d" % di, "ACTT%d" % grp],
                            writes=["P_A%d" % pb])
                    s.op("dve", lambda e, oc=oc, P=P, t0=t0: e.tensor_tensor(out=X[:, oc, t0:t0 + 512], in0=P,
                                                                           in1=X[:, oc, t0:t0 + 512], op=ALU.add),
                         reads=["P_A%d" % pb, xr], writes=[xr])
        if not last:
            return
        yo_v = yo.rearrange("(c p) t -> p c t", p=128)
        for tg in range(4):
            t0 = tg * 512
            xr = "X%d" % tg
            s.op("act", lambda e, t0=t0: e.activation(out=SQ[:, :, :], in_=X[:, :, t0:t0 + 512], func=AF.Square),
                 reads=[xr], writes=["SQ"])
            for c in range(8):
                s.op("pe", lambda e, c=c: e.matmul(P_SS, lhsT=ONES[:, :], rhs=SQ[:, c, :], start=(c == 0),
                                                   stop=(c == 7)), reads=["ONES", "SQ"], writes=["P_SS"])
            s.op("act", lambda e: e.activation(out=RSQ[:, :], in_=P_SS, func=AF.Sqrt, bias=EPSC[:, 0:1],
                                               scale=1.0 / D), reads=["P_SS", "EPSC"], writes=["RSQ"])
            s.op("dve", lambda e: e.reciprocal(out=RS[:, :], in_=RSQ[:, :]), reads=["RSQ"], writes=["RSd"])
            for c in range(8):
                yb = c % 2
                s.op("dve", lambda e, c=c, t0=t0, yb=yb: e.scalar_tensor_tensor(
                    out=G1[yb][:, :], in0=X[:, c, t0:t0 + 512], scalar=GFIN[:, c:c + 1], in1=RS[:, :],
                    op0=ALU.mult, op1=ALU.mult), reads=[xr, "GFIN", "RSd"], writes=["G1_%d" % yb])
                s.dma("sp", yo_v[:, c, t0:t0 + 512], G1[yb][:, :], reads=["G1_%d" % yb], writes=["o_y"],
                      slot="YS%d" % yb)

    def dbg_out():
        yo_v = yo.rearrange("(c p) t -> p c t", p=128)
        for tg in range(4):
            s.dma("sp", yo_v[:, :, tg * 512:(tg + 1) * 512], X[:, :, tg * 512:(tg + 1) * 512], reads=["X%d" % tg],
                  writes=["o_y%d" % tg], slot="dbgo%d" % tg)

    for l in range(depth):
        outs = phase_A(l)
        if stop == 'A':
            dbg_out(); break
        s.cc(ag1_in[l], ag1_out[l], reads=[o for o in outs if o.startswith("oq")], writes=["ag1o"], slot="cc1")
        s.cc(agf_in[l], agf_out[l], reads=[o for o in outs if o.startswith("of")], writes=["agfo"], slot="ccf")
        s.barrier()
        if stop == 'Acc':
            dbg_out(); break
        if stop in ('B0', 'B1'):
            phase_B(l, int(stop[1])); s.barrier(); dbg_out(); break
        onames = phase_B(l)
        if stop == 'B':
            dbg_out(); break
        s.cc(ag2_in[l], ag2_out[l], reads=onames, writes=["ag2o"], slot="cc2")
        s.barrier()
        if stop == 'Bcc':
            dbg_out(); break
        phase_C(l, l == depth - 1)
        s.barrier()
    s.emit()
    return nc


_NC = {}


def _pc(vec, nchunk):
    return np.ascontiguousarray(np.asarray(vec, np.float32).reshape(nchunk, 128).T)


def _consts():
    identf = np.eye(128, dtype=np.float32)
    identb = identf.astype(ml_dtypes.bfloat16)
    tri = (np.arange(128)[:, None] < np.arange(128)[None, :]).astype(np.float32)
    p = np.arange(128)[:, None, None]
    kbd = np.arange(4)[None, :, None]
    j = np.arange(512)[None, None, :]
    maskneg = np.where(kbd * 128 + p > j, NEG, 0.0).astype(np.float32).astype(ml_dtypes.bfloat16)
    return identf, identb, tri, maskneg


def _host_inputs(x, mix_norm_g, w_in, b_f, sgu_ln_g, sgu_ln_b, w_s, b_s, out_norm_g, w_out,
                 ffn_norm_g, w_gate_up, w_down, final_norm_g, depth):
    f32 = np.float32
    identf, identb, tri, maskneg = _consts()
    x = np.asarray(x, f32)
    L = depth
    ong = np.asarray(out_norm_g, f32)
    shared = dict(
        gmix=np.ascontiguousarray(np.stack([_pc(mix_norm_g[l], 8) for l in range(L)], axis=1)),
        w_in=np.ascontiguousarray(np.asarray(w_in, f32)[:L]),
        lng=np.ascontiguousarray(np.broadcast_to(np.asarray(sgu_ln_g, f32)[:L, None, :], (L, 128, 512))),
        lnb=np.ascontiguousarray(np.broadcast_to(np.asarray(sgu_ln_b, f32)[:L, None, :], (L, 128, 512))),
        wsT=np.ascontiguousarray(np.transpose(np.asarray(w_s, f32)[:L], (0, 3, 1, 2))),
        bsb=np.ascontiguousarray(np.broadcast_to(np.asarray(b_s, f32)[:L].reshape(L, 4, 2, 1, 128),
                                                 (L, 4, 2, 64, 128)).transpose(0, 2, 3, 1, 4).reshape(L, 128, 4, 128)),
        gos=np.ascontiguousarray(np.stack([_pc(ong[l, 512:], 4) for l in range(L)], axis=1)),
        goa=np.ascontiguousarray(np.stack([_pc(ong[l, :512], 4) for l in range(L)], axis=1)),
        gffn=np.ascontiguousarray(np.stack([_pc(ffn_norm_g[l], 8) for l in range(L)], axis=1)),
        gfin=_pc(final_norm_g, 8),
        w_out=np.ascontiguousarray(np.asarray(w_out, f32)[:L]),
        w_gu=np.ascontiguousarray(np.asarray(w_gate_up, f32)[:L]),
        w_dn=np.ascontiguousarray(np.asarray(w_down, f32)[:L]),
        identf=identf, identb=identb, tri=tri, maskneg=maskneg)
    in_maps = []
    bfa = np.asarray(b_f, f32)
    for r in range(NCORES):
        m = dict(shared)
        m["xT"] = np.ascontiguousarray(x[0, r * TS:(r + 1) * TS, :].T)
        m["bfh"] = np.ascontiguousarray(np.broadcast_to(bfa[:L, r][None, :], (128, L)))
        selm = np.zeros((4, 128, 64), f32)
        for d in range(64):
            feat = r * 64 + d
            selm[feat // 128, feat % 128, d] = 1.0
        m["sel"] = np.ascontiguousarray(selm.transpose(1, 0, 2)).astype(ml_dtypes.bfloat16)
        ohm = np.zeros((128, 8), f32)
        ohm[:, r] = 1.0
        m["oh"] = ohm
        in_maps.append(m)
    return in_maps


def kernel(x, mix_norm_g, w_in, b_f, sgu_ln_g, sgu_ln_b, w_s, b_s, out_norm_g, w_out,
           ffn_norm_g, w_gate_up, w_down, final_norm_g, _depth=DEPTH, _stop='full'):
    if (_depth, _stop) not in _NC:
        _NC[(_depth, _stop)] = build_fused(_depth, _stop)
    in_maps = _host_inputs(x, mix_norm_g, w_in, b_f, sgu_ln_g, sgu_ln_b, w_s, b_s, out_norm_g, w_out,
                           ffn_norm_g, w_gate_up, w_down, final_norm_g, _depth)
    if _stop not in ('full', 'C'):
        for m in in_maps:
            for k in ("w_out", "w_gu", "w_dn"):
                m.pop(k)
    res = run_bass_kernel_spmd(_NC[(_depth, _stop)], in_maps, core_ids=list(range(NCORES))).results
    out = np.concatenate([np.asarray(res[r]["yo"]).T for r in range(NCORES)], axis=0)[None].astype(np.float32)
    return out
```

```python
import numpy as np
import ml_dtypes
import concourse.bass as bass
import concourse.mybir as mybir
from concourse.bass_utils import run_bass_kernel_spmd

F32 = mybir.dt.float32
BF16 = mybir.dt.bfloat16
AF = mybir.ActivationFunctionType
ALU = mybir.AluOpType

NCORES = 8
D = 1024
S = 16384
DEPTH = 4
TS = S // NCORES
DH = 64
NH = 8
D_IN = 2568
DFF = 2816
EPS = 1e-6
NEG = -30000.0


class Sched:
    ENG = ("pe", "act", "dve", "pool", "sp")

    def __init__(self, nc):
        self.nc = nc
        self.ops = []
        self.by_eng = {e: [] for e in self.ENG}
        self.last_w = {}
        self.readers = {}
        self.slot_count = {}

    def op(self, eng, fn, reads=(), writes=(), dma_slot=None):
        idx = len(self.ops)
        deps = set()
        for r in reads:
            if r in self.last_w:
                deps.add(self.last_w[r])
        for w in writes:
            if w in self.last_w:
                deps.add(self.last_w[w])
            for rd in self.readers.get(w, ()):
                deps.add(rd)
        deps.discard(idx)
        rec = dict(eng=eng, fn=fn, deps=deps, dma=dma_slot, signal=False, ordinal=None)
        if dma_slot is not None:
            self.slot_count[dma_slot] = self.slot_count.get(dma_slot, 0) + 1
            rec["ordinal"] = self.slot_count[dma_slot]
        self.ops.append(rec)
        self.by_eng[eng].append(idx)
        for r in reads:
            self.readers.setdefault(r, []).append(idx)
        for w in writes:
            self.last_w[w] = idx
            self.readers[w] = []
        return idx

    def dma(self, q, out, in_, reads=(), writes=(), slot=None, **kw):
        assert slot is not None
        return self.op(q, lambda e: e.dma_start(out=out, in_=in_, **kw), reads, writes, dma_slot=slot)

    def emit(self):
        nc = self.nc
        ops = self.ops
        for i, o in enumerate(ops):
            keep = set()
            for d in o["deps"]:
                od = ops[d]
                if od["dma"] is None and od["eng"] == o["eng"] and o["dma"] is None and o["eng"] == "pe":
                    continue
                keep.add(d)
                if od["dma"] is None:
                    od["signal"] = True
            o["deps"] = keep
        sigcount = {}
        for e in self.ENG:
            c = 0
            for i in self.by_eng[e]:
                if ops[i]["dma"] is None and ops[i]["signal"]:
                    c += 1
                    sigcount[i] = c
        import contextlib
        with contextlib.ExitStack() as st:
            esem = {e: st.enter_context(nc.semaphore("s_" + e)) for e in self.ENG}
            ssem = {s: st.enter_context(nc.semaphore("d_%d" % k)) for k, s in enumerate(self.slot_count)}
            block = st.enter_context(nc.Block())

            def run(ename, eng):
                waited = {}
                for i in self.by_eng[ename]:
                    o = ops[i]
                    need = {}
                    for d in o["deps"]:
                        od = ops[d]
                        if od["dma"] is not None:
                            key = ("d", od["dma"])
                            val = 16 * od["ordinal"]
                        else:
                            key = ("e", od["eng"])
                            val = sigcount[d]
                        if val > need.get(key, 0):
                            need[key] = val
                    for key, val in need.items():
                        if waited.get(key, 0) >= val:
                            continue
                        waited[key] = val
                        sem = ssem[key[1]] if key[0] == "d" else esem[key[1]]
                        eng.wait_ge(sem, val)
                    ins = o["fn"](eng)
                    if o["dma"] is not None:
                        ins.then_inc(ssem[o["dma"]], 16)
                    elif o["signal"]:
                        ins.then_inc(esem[ename], 1)
                if ename == "sp":
                    for s, n in self.slot_count.items():
                        eng.wait_ge(ssem[s], 16 * n)

            @block.tensor
            def _(e):
                run("pe", e)

            @block.scalar
            def _(e):
                run("act", e)

            @block.vector
            def _(e):
                run("dve", e)

            @block.gpsimd
            def _(e):
                run("pool", e)

            @block.sync
            def _(e):
                run("sp", e)


def _new_nc():
    return bass.Bass("TRN2", target_bir_lowering=False)


def _din(nc, name, shape, dt):
    return nc.dram_tensor(name, list(shape), dt, kind="ExternalInput").ap()


def _dout(nc, name, shape, dt):
    return nc.dram_tensor(name, list(shape), dt, kind="ExternalOutput").ap()


def _sb(nc, name, shape, dt):
    return nc.alloc_sbuf_tensor(name, list(shape), dt)


def _ps(nc, name, shape, dt=F32):
    return nc.alloc_psum_tensor(name, list(shape), dt)


class Sched2(Sched):
    def __init__(self, nc):
        super().__init__(nc)
        self.bar = set()
        self.bar_pending = set()
        self.last_eng = {}
        self.last_slot = {}
        self.inc = {}

    def op(self, eng, fn, reads=(), writes=(), dma_slot=None, inc=16):
        idx = super().op(eng, fn, reads, writes, dma_slot)
        if eng in self.bar_pending:
            self.ops[idx]["deps"] |= self.bar
            self.bar_pending.discard(eng)
        if dma_slot is None:
            self.last_eng[eng] = idx
        else:
            self.last_slot[dma_slot] = idx
            self.inc[dma_slot] = inc
        return idx

    def barrier(self):
        self.bar = set(self.last_eng.values()) | set(self.last_slot.values())
        self.bar_pending = set(self.ENG)

    def cc(self, in_t, out_t, reads, writes, slot):
        rg = [list(range(NCORES))]
        return self.op("pool", lambda e: e.collective_compute(
            "AllGather", ALU.bypass, replica_groups=rg, ins=[in_t.ap().opt()], outs=[out_t.ap().opt()]),
            reads, writes, dma_slot=slot, inc=1)

    def emit(self):
        nc = self.nc
        ops = self.ops
        for i, o in enumerate(ops):
            keep = set()
            for d in o["deps"]:
                od = ops[d]
                if od["dma"] is None and od["eng"] == o["eng"] and o["dma"] is None and o["eng"] == "pe":
                    continue
                keep.add(d)
                if od["dma"] is None:
                    od["signal"] = True
            o["deps"] = keep
        sigcount = {}
        for e in self.ENG:
            c = 0
            for i in self.by_eng[e]:
                if ops[i]["dma"] is None and ops[i]["signal"]:
                    c += 1
                    sigcount[i] = c
        import contextlib
        with contextlib.ExitStack() as st:
            esem = {e: st.enter_context(nc.semaphore("s_" + e)) for e in self.ENG}
            ssem = {s: st.enter_context(nc.semaphore("d_%d" % k)) for k, s in enumerate(self.slot_count)}
            block = st.enter_context(nc.Block())

            def run(ename, eng):
                waited = {}
                for i in self.by_eng[ename]:
                    o = ops[i]
                    need = {}
                    for d in o["deps"]:
                        od = ops[d]
                        if od["dma"] is not None:
                            key = ("d", od["dma"])
                            val = self.inc[od["dma"]] * od["ordinal"]
                        else:
                            key = ("e", od["eng"])
                            val = sigcount[d]
                        if val > need.get(key, 0):
                            need[key] = val
                    for key, val in need.items():
                        if waited.get(key, 0) >= val:
                            continue
                        waited[key] = val
                        sem = ssem[key[1]] if key[0] == "d" else esem[key[1]]
                        eng.wait_ge(sem, val)
                    ins = o["fn"](eng)
                    if o["dma"] is not None:
                        ins.then_inc(ssem[o["dma"]], self.inc[o["dma"]])
                    elif o["signal"]:
                        ins.then_inc(esem[ename], 1)
                if ename == "sp":
                    for s_, n in self.slot_count.items():
                        eng.wait_ge(ssem[s_], self.inc[s_] * n)

            @block.tensor
            def _(e):
                run("pe", e)

            @block.scalar
            def _(e):
                run("act", e)

            @block.vector
            def _(e):
                run("dve", e)

            @block.gpsimd
            def _(e):
                run("pool", e)

            @block.sync
            def _(e):
                run("sp", e)


_DTB = {F32: 4, BF16: 2}


class Plan:
    def __init__(self, nc):
        self.nc = nc
        self.lo = (nc.sbuf_base + 31) // 32 * 32
        self.hi = nc.sbuf_top
        self.ptr = self.lo
        self.n = 0

    def alloc(self, name, shape, dt):
        nbytes = _DTB[dt]
        for d in shape[1:]:
            nbytes *= d
        nbytes = (nbytes + 31) // 32 * 32
        off = self.ptr
        self.ptr += nbytes
        assert self.ptr <= self.hi, (name, self.ptr, self.hi)
        self.n += 1
        return self.nc.alloc_sbuf_tensor_at("%s_%d" % (name, self.n), list(shape), dt, offset=off)

    def mark(self):
        return self.ptr

    def reset(self, mark):
        self.ptr = mark


def build_fused(depth=DEPTH, stop='full'):
    nc = _new_nc()
    big = stop in ('full', 'C')
    xT = _din(nc, "xT", [D, TS], F32)
    gmix = _din(nc, "gmix", [128, depth, 8], F32)
    w_in = _din(nc, "w_in", [depth, D, D_IN], F32)
    lng = _din(nc, "lng", [depth, 128, 512], F32)
    lnb = _din(nc, "lnb", [depth, 128, 512], F32)
    wsT = _din(nc, "wsT", [depth, 128, 8, 128], F32)
    bsb = _din(nc, "bsb", [depth, 128, 4, 128], F32)
    gos = _din(nc, "gos", [128, depth, 4], F32)
    goa = _din(nc, "goa", [128, depth, 4], F32)
    gffn = _din(nc, "gffn", [128, depth, 8], F32)
    gfin = _din(nc, "gfin", [128, 8], F32)
    w_out = _din(nc, "w_out", [depth, D, D], F32) if big else None
    w_gu = _din(nc, "w_gu", [depth, D, 2 * DFF], F32) if big else None
    w_dn = _din(nc, "w_dn", [depth, DFF, D], F32) if big else None
    bfh = _din(nc, "bfh", [128, depth], F32)
    sel = _din(nc, "sel", [128, 4, 64], BF16)
    oh = _din(nc, "oh", [128, 8], F32)
    identf = _din(nc, "identf", [128, 128], F32)
    identb = _din(nc, "identb", [128, 128], BF16)
    tri = _din(nc, "tri", [128, 128], F32)
    maskneg = _din(nc, "maskneg", [128, 4, 512], BF16)
    yo = _dout(nc, "yo", [D, TS], F32)

    ag1_in = [nc.dram_tensor("ag1_in%d" % l, [1536, TS], BF16) for l in range(depth)]
    ag1_out = [nc.dram_tensor("ag1_out%d" % l, [NCORES * 1536, TS], BF16) for l in range(depth)]
    agf_in = [nc.dram_tensor("agf_in%d" % l, [8, TS], F32) for l in range(depth)]
    agf_out = [nc.dram_tensor("agf_out%d" % l, [NCORES * 8, TS], F32) for l in range(depth)]
    ag2_in = [nc.dram_tensor("ag2_in%d" % l, [512, TS], F32) for l in range(depth)]
    ag2_out = [nc.dram_tensor("ag2_out%d" % l, [NCORES * 512, TS], F32) for l in range(depth)]
    sg_scr = [nc.dram_tensor("sg_scr%d" % l, [512, TS], BF16) for l in range(depth)]
    crow = [nc.dram_tensor("crow%d" % l, [3, S], BF16) for l in range(depth)]

    pl = Plan(nc)
    X = pl.alloc("X", [128, 8, TS], F32)
    ONES = pl.alloc("ONES", [128, 128], BF16)
    ONEF = pl.alloc("ONEF", [128, 128], F32)
    IDF = pl.alloc("IDF", [128, 128], F32)
    IDB = pl.alloc("IDB", [128, 128], BF16)
    TRI = pl.alloc("TRI", [128, 128], F32)
    MN = pl.alloc("MN", [128, 4, 512], BF16)
    SEL = pl.alloc("SEL", [128, 4, 64], BF16)
    OH = pl.alloc("OH", [128, 8], F32)
    EPSC = pl.alloc("EPSC", [128, 1], F32)
    GMIX = pl.alloc("GMIX", [128, depth, 8], F32)
    GOS = pl.alloc("GOS", [128, depth, 4], F32)
    GOA = pl.alloc("GOA", [128, depth, 4], F32)
    GFFN = pl.alloc("GFFN", [128, depth, 8], F32)
    GFIN = pl.alloc("GFIN", [128, 8], F32)
    BFH = pl.alloc("BFH", [128, depth], F32)
    arena = pl.mark()

    PB = [nc.alloc_psum_tensor("PB%d" % i, [128, 1024], F32) for i in range(4)]

    s = Sched2(nc)
    for nm, t, src in [("IDF", IDF, identf), ("IDB", IDB, identb), ("TRI", TRI, tri), ("OH", OH, oh),
                       ("GFIN", GFIN, gfin), ("BFH", BFH, bfh)]:
        s.dma("sp", t[:, :], src, writes=[nm], slot=nm)
    for nm, t, src in [("MN", MN, maskneg), ("SEL", SEL, sel), ("GMIX", GMIX, gmix), ("GOS", GOS, gos),
                       ("GOA", GOA, goa), ("GFFN", GFFN, gffn)]:
        s.dma("sp", t[:, :, :], src, writes=[nm], slot=nm)
    xT_v = xT.rearrange("(c p) t -> p c t", p=128)
    for tg in range(4):
        s.dma("sp", X[:, :, tg * 512:(tg + 1) * 512], xT_v[:, :, tg * 512:(tg + 1) * 512],
              writes=["X%d" % tg], slot="X%d" % tg)
    s.op("dve", lambda e: e.memset(ONES[:, :], 1.0), writes=["ONES"])
    s.op("dve", lambda e: e.memset(EPSC[:, :], EPS), writes=["EPSC"])
    s.op("pool", lambda e: e.memset(ONEF[:, :], 1.0), writes=["ONEF"])

    pl.reset(arena)
    A = dict(
        W=pl.alloc("W", [128, 8, D_IN], BF16), LNG=pl.alloc("LNG", [128, 512], F32),
        LNB=pl.alloc("LNB", [128, 512], F32), WS=pl.alloc("WS", [128, 8, 128], BF16),
        BS=pl.alloc("BS", [128, 4, 128], F32), SQ=pl.alloc("SQ", [128, 8, 512], BF16),
        RSQ=pl.alloc("RSQ", [128, 512], F32), RSTD=pl.alloc("RSTD", [128, 512], F32),
        XB=pl.alloc("XB", [128, 8, 512], BF16),
        STG=[pl.alloc("STG%d" % i, [128, 512], BF16) for i in range(3)],
        FST=pl.alloc("FST", [8, 512], F32), ZR=[pl.alloc("ZR%d" % i, [128, 512], F32) for i in range(2)],
        ZU=pl.alloc("ZU", [128, 4, 512], F32), ZV=pl.alloc("ZV", [128, 4, 512], F32),
        ST6=pl.alloc("ST6", [128, 6], F32), MV=pl.alloc("MV", [128, 2], F32), SD=pl.alloc("SD", [128, 1], F32),
        RL=pl.alloc("RL", [128, 1], F32), ZN=pl.alloc("ZN", [128, 512], F32), ZN2=pl.alloc("ZN2", [128, 512], F32),
        ZNB=pl.alloc("ZNB", [128, 512], BF16), SG1=pl.alloc("SG1", [128, 4, 128], F32),
        SGU=pl.alloc("SGU", [128, 4, 512], F32), SQ2=pl.alloc("SQ2", [128, 4, 512], BF16),
        RSQ2=pl.alloc("RSQ2", [128, 512], F32), RS2=pl.alloc("RS2", [128, 512], F32),
        SGN=pl.alloc("SGN", [128, 4, 512], BF16),
        WST=[pl.alloc("WST%d" % i, [128, 8, 256], F32) for i in range(2)])
    endA = pl.mark()
    pl.reset(arena)
    B = dict(
        QA=pl.alloc("QA", [67, S], BF16), KA=pl.alloc("KA", [67, S], BF16), V=pl.alloc("V", [128, 128, 66], BF16),
        PT=[pl.alloc("PT%d" % i, [128, 1024], BF16) for i in range(3)],
        LQ=[pl.alloc("LQ%d" % i, [128, 12, 256], BF16) for i in range(2)],
        FA=pl.alloc("FA", [128, 8, 128], F32),
        OSB=[pl.alloc("OSB%d" % i, [65, 512], F32) for i in range(2)],
        OTS=[pl.alloc("OTS%d" % i, [64, 512], F32) for i in range(2)],
        HI=pl.alloc("HI", [128, 3, 128], BF16))
    for nm in ["Fb", "Y", "A", "E", "L", "M", "LF", "SC", "C", "NEGC", "HF", "R1", "R2"]:
        B[nm] = pl.alloc(nm, [128, 128], F32)
    B["OFF"] = pl.alloc("OFF", [128, 2], F32)
    endB = pl.mark()
    pl.reset(arena)
    m_wo = pl.mark()
    C = dict(WO=pl.alloc("WO", [128, 8, D], BF16))
    m_at = pl.mark()
    C.update(
        AT=pl.alloc("AT", [128, 4, 512], F32),
        SQ=pl.alloc("SQc", [128, 8, 512], BF16), RSQ=pl.alloc("RSQc", [128, 512], F32),
        RS=pl.alloc("RSc", [128, 512], F32), RSTD=pl.alloc("RSTDc", [128, 1024], F32),
        XB=pl.alloc("XBc", [128, 8, 1024], BF16),
        PAN=[pl.alloc("PAN%d" % i, [128, 2048], BF16) for i in range(2)],
        DPN=[pl.alloc("DPN%d" % i, [128, 22, 128], BF16) for i in range(2)],
        G1=[pl.alloc("G1_%d" % i, [128, 512], F32) for i in range(2)],
        U1=[pl.alloc("U1_%d" % i, [128, 512], F32) for i in range(2)])
    actt_mark = pl.mark()
    C["ACTT"] = pl.alloc("ACTT", [128, 22, 1024], BF16)
    pans0 = pl.alloc("PANS0", [128, 2048], F32)
    endC = pl.mark()
    pl.reset(actt_mark)
    C["TMP"] = [pl.alloc("TMP%d" % i, [128, 4, 512], F32) for i in range(5)]
    assert pl.mark() <= endC
    pl.reset(m_at)
    C["PANS"] = [pans0, pl.alloc("PANS1", [128, 2048], F32)]
    pl.reset(m_wo)
    C["DPNS"] = [pl.alloc("DPNS%d" % i, [128, 11, 128], F32) for i in range(2)]
    assert pl.mark() <= m_at
    print("SBUF plan: arena", arena, "endA", endA, "endB", endB, "endC", endC, "hi", pl.hi)

    def phase_A(l):
        W, LNG, LNB, WS, BS, SQ, RSQ, RSTD, XB = (A[k] for k in ["W", "LNG", "LNB", "WS", "BS", "SQ", "RSQ", "RSTD", "XB"])
        STG, FST, ZR, ZU, ZV, ST6, MV, SD, RL = (A[k] for k in ["STG", "FST", "ZR", "ZU", "ZV", "ST6", "MV", "SD", "RL"])
        ZN, ZN2, ZNB, SG1, SGU, SQ2, RSQ2, RS2, SGN = (A[k] for k in ["ZN", "ZN2", "ZNB", "SG1", "SGU", "SQ2", "RSQ2", "RS2", "SGN"])
        P_SS = PB[0][:, 0:512]
        P_S2 = PB[0][:, 512:1024]
        P_PJ = [PB[1][:, 0:512], PB[1][:, 512:1024]]
        P_ZT = PB[2][:, 0:512]
        P_MX = PB[2][:, 512:1024].rearrange("p (a b) -> p a b", a=4)
        qkv_d = ag1_in[l].ap()
        f_d = agf_in[l].ap()
        sg_d = sg_scr[l].ap()
        s.dma("sp", LNG[:, :], lng[l], writes=["LNG"], slot="LNG")
        s.dma("sp", LNB[:, :], lnb[l], writes=["LNB"], slot="LNB")
        s.dma("sp", BS[:, :, :], bsb[l], writes=["BS"], slot="BS")
        w_v = w_in[l].rearrange("(c p) n -> p c n", p=128)
        WST = A["WST"]
        panels = [(i * 256, min((i + 1) * 256, D_IN)) for i in range(11)]

        def load_w():
            for pi, (a, b) in enumerate(panels):
                st = WST[pi % 2]
                s.dma("sp", st[:, :, 0:b - a], w_v[:, :, a:b], writes=["WST%d" % (pi % 2)], slot="WST%d" % (pi % 2))
                s.op("act", lambda e, st=st, a=a, b=b: e.activation(out=W[:, :, a:b], in_=st[:, :, 0:b - a],
                                                                   func=AF.Copy),
                     reads=["WST%d" % (pi % 2)], writes=["W%d" % pi])
        s.dma("pool", WS[:, :, :], wsT[l], writes=["WS"], slot="WS")
        s.op("pool", lambda e: e.memset(WS[64:128, :, 0:64], 0.0), reads=[], writes=["WS"])

        def wpanel(col):
            for pi, (a, b) in enumerate(panels):
                if a <= col < b:
                    return "W%d" % pi
            raise ValueError

        pj = [0]
        stg = [0]
        outs = []
        for tg in range(4):
            t0 = tg * 512
            xs = X[:, :, t0:t0 + 512]
            xr = "X%d" % tg
            s.op("act", lambda e, xs=xs: e.activation(out=SQ[:, :, :], in_=xs, func=AF.Square),
                 reads=[xr], writes=["SQ"])
            for c in range(8):
                s.op("pe", lambda e, c=c: e.matmul(P_SS, lhsT=ONES[:, :], rhs=SQ[:, c, :],
                                                   start=(c == 0), stop=(c == 7)),
                     reads=["ONES", "SQ"], writes=["P_SS"])
            s.op("act", lambda e: e.activation(out=RSQ[:, :], in_=P_SS, func=AF.Sqrt,
                                               bias=EPSC[:, 0:1], scale=1.0 / D),
                 reads=["P_SS", "EPSC"], writes=["RSQ"])
            s.op("dve", lambda e: e.reciprocal(out=RSTD[:, :], in_=RSQ[:, :]), reads=["RSQ"], writes=["RSTD"])
            if tg == 0:
                load_w()
            for c in range(8):
                eng = "dve" if c % 2 == 0 else "pool"
                s.op(eng, lambda e, c=c, xs=xs: e.tensor_scalar(out=XB[:, c, :], in0=xs[:, c, :],
                                                               scalar1=GMIX[:, l, c:c + 1], scalar2=None,
                                                               op0=ALU.mult),
                     reads=[xr, "GMIX"], writes=["XB%d" % c])
            xbr = ["XB%d" % c for c in range(8)]
            for oc in range(12):
                pb = pj[0] % 2
                pj[0] += 1
                P = P_PJ[pb]
                for c in range(8):
                    s.op("pe", lambda e, c=c, oc=oc, P=P: e.matmul(P, lhsT=W[:, c, oc * 128:(oc + 1) * 128],
                                                                 rhs=XB[:, c, :], start=(c == 0), stop=(c == 7)),
                         reads=[wpanel(oc * 128)] + xbr, writes=["P_PJ%d" % pb])
                sb = stg[0] % 3
                stg[0] += 1
                sc = 0.125 if oc < 4 else 1.0
                s.op("dve", lambda e, P=P, sb=sb, sc=sc: e.scalar_tensor_tensor(
                    out=STG[sb][:, :], in0=P, scalar=sc, in1=RSTD[:, :], op0=ALU.mult, op1=ALU.mult),
                     reads=["P_PJ%d" % pb, "RSTD"], writes=["STG%d" % sb])
                on = "oq%d_%d_%d" % (l, tg, oc)
                outs.append(on)
                s.dma("sp", qkv_d[oc * 128:(oc + 1) * 128, t0:t0 + 512], STG[sb][:, :],
                      reads=["STG%d" % sb], writes=[on], slot="STG%d" % sb)
            pb = pj[0] % 2
            pj[0] += 1
            P = P_PJ[pb]
            for c in range(8):
                s.op("pe", lambda e, c=c, P=P: e.matmul(P[0:8, :], lhsT=W[:, c, 1536:1544], rhs=XB[:, c, :],
                                                       start=(c == 0), stop=(c == 7)),
                     reads=[wpanel(1536)] + xbr, writes=["P_PJ%d" % pb])
            s.op("dve", lambda e, P=P: e.tensor_tensor(out=FST[:, :], in0=P[0:8, :], in1=RSTD[0:8, :], op=ALU.mult),
                 reads=["P_PJ%d" % pb, "RSTD"], writes=["FST"])
            on = "of%d_%d" % (l, tg)
            outs.append(on)
            s.dma("sp", f_d[:, t0:t0 + 512], FST[:, :], reads=["FST"], writes=[on], slot="FST")
            for zc in range(8):
                pb = pj[0] % 2
                pj[0] += 1
                P = P_PJ[pb]
                col = 1544 + zc * 128
                for c in range(8):
                    s.op("pe", lambda e, c=c, col=col, P=P: e.matmul(P, lhsT=W[:, c, col:col + 128],
                                                                   rhs=XB[:, c, :], start=(c == 0), stop=(c == 7)),
                         reads=[wpanel(col), wpanel(col + 127)] + xbr, writes=["P_PJ%d" % pb])
                zb = zc % 2
                s.op("dve", lambda e, P=P, zb=zb: e.tensor_tensor(out=ZR[zb][:, :], in0=P, in1=RSTD[:, :],
                                                                op=ALU.mult),
                     reads=["P_PJ%d" % pb, "RSTD"], writes=["ZR%d" % zb])
                dst = ZU[:, zc, :] if zc < 4 else ZV[:, zc - 4, :]
                dr = ("ZU%d" % zc) if zc < 4 else ("ZV%d" % (zc - 4))
                s.op("act", lambda e, zb=zb, dst=dst: e.activation(out=dst, in_=ZR[zb][:, :], func=AF.Gelu),
                     reads=["ZR%d" % zb], writes=[dr])
            zur = ["ZU%d" % c for c in range(4)]
            for tt in range(4):
                a0 = tt * 128
                for c4 in range(4):
                    s.op("pe", lambda e, c4=c4, a0=a0: e.transpose(out=P_ZT[:, c4 * 128:(c4 + 1) * 128],
                                                                 in_=ZV[:, c4, a0:a0 + 128], identity=IDF[:, :]),
                         reads=["ZV%d" % c4, "IDF"], writes=["P_ZT"])
                s.op("dve", lambda e: e.bn_stats(out=ST6[:, :], in_=P_ZT), reads=["P_ZT"], writes=["ST6"])
                s.op("dve", lambda e: e.bn_aggr(out=MV[:, :], in_=ST6[:, :]), reads=["ST6"], writes=["MV"])
                s.op("act", lambda e: e.activation(out=SD[:, :], in_=MV[:, 1:2], func=AF.Sqrt, bias=EPSC[:, 0:1],
                                                   scale=1.0),
                     reads=["MV", "EPSC"], writes=["SD"])
                s.op("dve", lambda e: e.reciprocal(out=RL[:, :], in_=SD[:, :]), reads=["SD"], writes=["RL"])
                s.op("dve", lambda e: e.tensor_scalar(out=ZN[:, :], in0=P_ZT, scalar1=MV[:, 0:1],
                                                      scalar2=RL[:, 0:1], op0=ALU.subtract, op1=ALU.mult),
                     reads=["P_ZT", "MV", "RL"], writes=["ZN"])
                s.op("pool", lambda e: e.tensor_tensor(out=ZN2[:, :], in0=ZN[:, :], in1=LNG[:, :], op=ALU.mult),
                     reads=["ZN", "LNG"], writes=["ZN2"])
                s.op("dve", lambda e: e.tensor_tensor(out=ZNB[:, :], in0=ZN2[:, :], in1=LNB[:, :], op=ALU.add),
                     reads=["ZN2", "LNB"], writes=["ZNB"])
                for g in range(8):
                    po = (g % 2) * 64
                    s.op("pe", lambda e, g=g, po=po: e.matmul(P_MX[po:po + 64, g // 2, :],
                                                            lhsT=ZNB[:, g * 64:(g + 1) * 64], rhs=WS[:, g, :],
                                                            start=True, stop=True),
                         reads=["ZNB", "WS"], writes=["P_MX"])
                s.op("dve", lambda e: e.tensor_tensor(out=SG1[:, :, :], in0=P_MX, in1=BS[:, :, :], op=ALU.add),
                     reads=["P_MX", "BS"], writes=["SG1"])
                s.op("pool", lambda e, a0=a0: e.tensor_tensor(out=SGU[:, :, a0:a0 + 128], in0=SG1[:, :, :],
                                                            in1=ZU[:, :, a0:a0 + 128], op=ALU.mult),
                     reads=["SG1"] + zur, writes=["SGU"])
            s.op("act", lambda e: e.activation(out=SQ2[:, :, :], in_=SGU[:, :, :], func=AF.Square),
                 reads=["SGU"], writes=["SQ2"])
            for c in range(4):
                s.op("pe", lambda e, c=c: e.matmul(P_S2, lhsT=ONES[:, :], rhs=SQ2[:, c, :],
                                                   start=(c == 0), stop=(c == 3)),
                     reads=["ONES", "SQ2"], writes=["P_S2"])
            s.op("act", lambda e: e.activation(out=RSQ2[:, :], in_=P_S2, func=AF.Sqrt, bias=EPSC[:, 0:1],
                                               scale=1.0 / 512),
                 reads=["P_S2", "EPSC"], writes=["RSQ2"])
            s.op("dve", lambda e: e.reciprocal(out=RS2[:, :], in_=RSQ2[:, :]), reads=["RSQ2"], writes=["RS2"])
            for c in range(4):
                s.op("dve", lambda e, c=c: e.scalar_tensor_tensor(out=SGN[:, c, :], in0=SGU[:, c, :],
                                                                scalar=GOS[:, l, c:c + 1], in1=RS2[:, :],
                                                                op0=ALU.mult, op1=ALU.mult),
                     reads=["SGU", "GOS", "RS2"], writes=["SGN"])
            s.dma("sp", sg_d.rearrange("(c p) t -> p c t", p=128)[:, :, t0:t0 + 512], SGN[:, :, :],
                  reads=["SGN"], writes=["sg_scr"], slot="SGN")
        return outs

    def phase_B(l, upto=9):
        QA, KA, V, PT, LQ, FA, OSB, OTS, HI = (B[k] for k in ["QA", "KA", "V", "PT", "LQ", "FA", "OSB", "OTS", "HI"])
        Fb, Y, A_, E, L, M, LF, SC, C_, NEGC, HF, R1, R2, OFF = (B[k] for k in
            ["Fb", "Y", "A", "E", "L", "M", "LF", "SC", "C", "NEGC", "HF", "R1", "R2", "OFF"])
        PS_S = [PB[0], PB[1]]
        PS_O = [[PB[2][:, 0:512], PB[2][:, 512:1024]], [PB[3][:, 0:512], PB[3][:, 512:1024]]]
        a1 = ag1_out[l].ap()
        af = agf_out[l].ap()
        o_d = ag2_in[l].ap()
        crow_ap = crow[l].ap()
        s.op("pool", lambda e: e.memset(V[:, :, 64:65], 1.0), writes=["Vo"])
        s.op("pool", lambda e: e.memset(KA[64:67, :], 1.0), writes=["KAo"])
        for r in range(NCORES):
            s.dma("sp", FA[r * 16:(r + 1) * 16, :, :], af[r * 8:(r + 1) * 8, :].rearrange("h (g j) -> g h j", j=128),
                  writes=["FA%d" % r], slot="FA%d" % r)
        far = ["FA%d" % r for r in range(NCORES)]
        s.op("dve", lambda e: e.tensor_scalar(out=Fb[:, :], in0=FA[:, 0, :], scalar1=OH[:, 0:1], scalar2=None,
                                              op0=ALU.mult), reads=far + ["OH"], writes=["Fb"])
        for h in range(1, 8):
            s.op("dve", lambda e, h=h: e.scalar_tensor_tensor(out=Fb[:, :], in0=FA[:, h, :], scalar=OH[:, h:h + 1],
                                                            in1=Fb[:, :], op0=ALU.mult, op1=ALU.add),
                 reads=far + ["OH", "Fb"], writes=["Fb"])
        s.op("dve", lambda e: e.tensor_scalar(out=Y[:, :], in0=Fb[:, :], scalar1=BFH[:, l:l + 1], scalar2=-1.0,
                                              op0=ALU.add, op1=ALU.mult), reads=["Fb", "BFH"], writes=["Y"])
        s.op("dve", lambda e: e.tensor_scalar(out=E[:, :], in0=Y[:, :], scalar1=-1.0, scalar2=None, op0=ALU.mult),
             reads=["Y"], writes=["E"])
        s.op("dve", lambda e: e.tensor_tensor(out=A_[:, :], in0=Y[:, :], in1=E[:, :], op=ALU.max),
             reads=["Y", "E"], writes=["A"])
        s.op("act", lambda e: e.activation(out=E[:, :], in_=A_[:, :], func=AF.Exp, scale=-1.0),
             reads=["A"], writes=["E"])
        s.op("act", lambda e: e.activation(out=L[:, :], in_=E[:, :], func=AF.Ln, bias=ONEF[:, 0:1], scale=1.0),
             reads=["E", "ONEF"], writes=["L"])
        s.op("dve", lambda e: e.tensor_single_scalar(out=M[:, :], in_=Y[:, :], scalar=0.0, op=ALU.max),
             reads=["Y"], writes=["M"])
        s.op("dve", lambda e: e.scalar_tensor_tensor(out=LF[:, :], in0=M[:, :], scalar=-1.0, in1=L[:, :],
                                                     op0=ALU.mult, op1=ALU.subtract), reads=["M", "L"], writes=["LF"])
        s.op("dve", lambda e: e.tensor_tensor_scan(out=SC[:, :], data0=ONEF[:, :], data1=LF[:, :], initial=0.0,
                                                   op0=ALU.mult, op1=ALU.add), reads=["ONEF", "LF"], writes=["SC"])
        PO = PS_O[0][0]
        s.op("pe", lambda e: e.matmul(PO[:, 0:2], lhsT=TRI[:, :], rhs=SC[:, 126:128], start=True, stop=True),
             reads=["TRI", "SC"], writes=["PS_O00"])
        s.op("dve", lambda e: e.tensor_copy(out=OFF[:, :], in_=PO[:, 0:2]), reads=["PS_O00"], writes=["OFF"])
        s.op("dve", lambda e: e.tensor_scalar(out=C_[:, :], in0=SC[:, :], scalar1=OFF[:, 1:2], scalar2=None,
                                              op0=ALU.add), reads=["SC", "OFF"], writes=["C"])
        PO2 = PS_O[0][1]
        s.op("pe", lambda e: e.transpose(out=PO2[:, 0:128], in_=C_[:, :], identity=IDF[:, :]),
             reads=["C", "IDF"], writes=["PS_O01"])
        s.op("dve", lambda e: e.tensor_scalar(out=NEGC[:, :], in0=PO2[:, 0:128], scalar1=-1.0, scalar2=None,
                                              op0=ALU.mult), reads=["PS_O01"], writes=["NEGC"])
        s.op("dve", lambda e: e.tensor_copy(out=HI[:, 0, :], in_=C_[:, :]), reads=["C"], writes=["HI0"])
        s.op("dve", lambda e: e.tensor_copy(out=HF[:, :], in_=HI[:, 0, :]), reads=["HI0"], writes=["HF"])
        s.op("dve", lambda e: e.tensor_tensor(out=R1[:, :], in0=C_[:, :], in1=HF[:, :], op=ALU.subtract),
             reads=["C", "HF"], writes=["R1"])
        s.op("dve", lambda e: e.tensor_copy(out=HI[:, 1, :], in_=R1[:, :]), reads=["R1"], writes=["HI1"])
        s.op("dve", lambda e: e.tensor_copy(out=HF[:, :], in_=HI[:, 1, :]), reads=["HI1"], writes=["HF"])
        s.op("dve", lambda e: e.tensor_tensor(out=R2[:, :], in0=R1[:, :], in1=HF[:, :], op=ALU.subtract),
             reads=["R1", "HF"], writes=["R2"])
        s.op("dve", lambda e: e.tensor_copy(out=HI[:, 2, :], in_=R2[:, :]), reads=["R2"], writes=["HI2"])
        for r in range(3):
            s.dma("sp", crow_ap[r:r + 1, :].rearrange("o (p j) -> (o p) j", p=128), HI[:, r, :],
                  reads=["HI%d" % r], writes=["crow%d" % r], slot="crow%d" % r)
        s.dma("sp", QA[64:67, :], crow_ap[:, :], reads=["crow0", "crow1", "crow2"], writes=["QAc"], slot="QAc")

        if upto == 0:
            return []
        for hg in range(64):
            r = hg // 8
            col0 = (hg % 8) * 256
            tok0 = hg * 256
            lb = hg % 2
            s.dma("sp", LQ[lb][:, :, :],
                  a1[r * 1536:(r + 1) * 1536, col0:col0 + 256].rearrange("(c p) t -> p c t", p=128),
                  writes=["LQ%d" % lb], slot="LQ%d" % lb)
            PSq = PS_S[lb][0:64, 0:256]
            PSk = PS_S[lb][0:64, 512:768]
            PSv = PB[3][:, lb * 512:lb * 512 + 128].rearrange("p (a b) -> p a b", a=2)
            for c in range(4):
                s.op("pe", lambda e, c=c, lb=lb, PSq=PSq: e.matmul(PSq, lhsT=SEL[:, c, :], rhs=LQ[lb][:, c, :],
                                                                 start=(c == 0), stop=(c == 3)),
                     reads=["SEL", "LQ%d" % lb], writes=["psq%d" % lb])
            for c in range(4):
                s.op("pe", lambda e, c=c, lb=lb, PSk=PSk: e.matmul(PSk, lhsT=SEL[:, c, :], rhs=LQ[lb][:, 4 + c, :],
                                                                 start=(c == 0), stop=(c == 3)),
                     reads=["SEL", "LQ%d" % lb], writes=["psk%d" % lb])
            s.op("act", lambda e, lb=lb, PSq=PSq, tok0=tok0: e.activation(out=QA[0:64, tok0:tok0 + 256], in_=PSq,
                                                                        func=AF.Copy),
                 reads=["psq%d" % lb], writes=["QAq"])
            s.op("dve", lambda e, lb=lb, PSk=PSk, tok0=tok0: e.tensor_copy(out=KA[0:64, tok0:tok0 + 256], in_=PSk),
                 reads=["psk%d" % lb], writes=["KAk"])
            for tt in range(2):
                for c in range(4):
                    s.op("pe", lambda e, c=c, lb=lb, tt=tt, PSv=PSv: e.matmul(
                        PSv[:, tt, :], lhsT=LQ[lb][:, 8 + c, tt * 128:(tt + 1) * 128], rhs=SEL[:, c, :],
                        start=(c == 0), stop=(c == 3)), reads=["SEL", "LQ%d" % lb], writes=["psv%d" % lb])
            blk = hg * 2
            s.op("dve", lambda e, lb=lb, PSv=PSv, blk=blk: e.tensor_copy(out=V[:, blk:blk + 2, 0:64], in_=PSv),
                 reads=["psv%d" % lb], writes=["Vv"])

        if upto == 1:
            return []
        items = []
        for P in range(16):
            for kb in range(8 * P + 8):
                items.append((P, kb))
        n = len(items)
        pending_evac = []
        first = [True]

        def emit_qk(it):
            P, kb = items[it]
            sb = it % 2
            SS = PS_S[sb]
            subs = [0, 1] if kb <= 8 * P + 3 else [1]
            extra = ["psq0", "psq1", "psk0", "psk1", "psv0", "psv1"] if it < 2 else []
            for sub in subs:
                q0 = P * 1024 + sub * 512
                kbd = kb - (8 * P + 4 * sub)
                diag = 0 <= kbd <= 3
                s.op("pe", lambda e, SS=SS, sub=sub, q0=q0, kb=kb, diag=diag: e.matmul(
                    SS[:, sub * 512:(sub + 1) * 512], lhsT=KA[0:67, kb * 128:(kb + 1) * 128],
                    rhs=QA[0:67, q0:q0 + 512], start=True, stop=(not diag)),
                    reads=["KAk", "KAo", "QAq", "QAc"], writes=["PS_S%d" % sb] + extra)
                if diag:
                    s.op("pe", lambda e, SS=SS, sub=sub, kbd=kbd: e.matmul(
                        SS[:, sub * 512:(sub + 1) * 512], lhsT=IDB[:, :], rhs=MN[:, kbd, :], start=False, stop=True),
                        reads=["IDB", "MN"], writes=["PS_S%d" % sb])
            lo = subs[0] * 512
            pb = it % 3
            s.op("act", lambda e, SS=SS, lo=lo, pb=pb, kb=kb: e.activation(
                out=PT[pb][:, lo:1024], in_=SS[:, lo:1024], func=AF.Exp, bias=NEGC[:, kb:kb + 1], scale=1.0),
                reads=["PS_S%d" % sb, "NEGC"], writes=["PT%d" % pb])

        def emit_pv(it):
            P, kb = items[it]
            ob = P % 2
            pb = it % 3
            subs = [0, 1] if kb <= 8 * P + 3 else [1]
            for sub in subs:
                last = 8 * P + 4 * sub + 3
                PO_ = PS_O[ob][sub]
                s.op("pe", lambda e, PO_=PO_, kb=kb, pb=pb, sub=sub, last=last: e.matmul(
                    PO_[0:65, :], lhsT=V[:, kb, 0:65], rhs=PT[pb][:, sub * 512:(sub + 1) * 512],
                    start=(kb == 0), stop=(kb == last)), reads=["Vv", "Vo", "PT%d" % pb],
                    writes=["PS_O%d%d" % (ob, sub)])
                if kb == last:
                    pending_evac.append([P, sub, it + 3, 0])

        o_names = []

        def emit_evac(P, sub, stage):
            ob = P % 2
            PO_ = PS_O[ob][sub]
            por = "PS_O%d%d" % (ob, sub)
            q0 = P * 1024 + sub * 512
            if stage == 0:
                s.op("dve", lambda e: e.tensor_copy(out=OSB[sub][:, :], in_=PO_[0:65, :]), reads=[por],
                     writes=["OSB%d" % sub])
                s.op("dve", lambda e: e.reciprocal(out=OSB[sub][64:65, :], in_=OSB[sub][64:65, :]),
                     reads=["OSB%d" % sub], writes=["OSB%d" % sub])
            else:
                s.op("pe", lambda e: e.matmul(PO_[0:64, :], lhsT=ONEF[64:65, 0:64], rhs=OSB[sub][64:65, :],
                                              start=True, stop=True), reads=["ONEF", "OSB%d" % sub], writes=[por])
                s.op("dve", lambda e: e.tensor_tensor(out=OTS[sub][:, :], in0=OSB[sub][0:64, :], in1=PO_[0:64, :],
                                                      op=ALU.mult), reads=["OSB%d" % sub, por], writes=["OTS%d" % sub])
                sh = q0 // TS
                c0 = q0 % TS
                on = "oo%d_%d_%d" % (l, P, sub)
                o_names.append(on)
                s.dma("sp", o_d[sh * 64:(sh + 1) * 64, c0:c0 + 512], OTS[sub][:, :], reads=["OTS%d" % sub],
                      writes=[on], slot="OTS%d" % sub)

        for it in range(n + 1):
            if it < n:
                emit_qk(it)
            if it >= 1:
                emit_pv(it - 1)
            for ev in list(pending_evac):
                if ev[3] == 0:
                    emit_evac(ev[0], ev[1], 0)
                    ev[3] = 1
                elif it >= ev[2] or it == n:
                    emit_evac(ev[0], ev[1], 1)
                    pending_evac.remove(ev)
        for ev in list(pending_evac):
            if ev[3] == 0:
                emit_evac(ev[0], ev[1], 0)
            emit_evac(ev[0], ev[1], 1)
        return o_names

    def phase_C(l, last):
        WO, AT, SQ, RSQ, RS, RSTD, XB, PAN, DPN, G1, U1, ACTT, TMP = (C[k] for k in
            ["WO", "AT", "SQ", "RSQ", "RS", "RSTD", "XB", "PAN", "DPN", "G1", "U1", "ACTT", "TMP"])
        PANS, DPNS = C["PANS"], C["DPNS"]
        PANv = [t[:, :].rearrange("p (c u n) -> p c u n", c=8, u=2) for t in PAN]
        PANSv = [t[:, :].rearrange("p (c u n) -> p c u n", c=8, u=2) for t in PANS]
        PANSq = PANS[0][:, :].rearrange("p (c n) -> p c n", c=8)
        MG = XB[:, :, 0:512]
        P_SS = PB[0][:, 0:512]
        P_A = [PB[1][:, 0:512], PB[1][:, 512:1024]]
        P_G = [PB[2][:, 0:512], PB[2][:, 512:1024]]
        P_U = [PB[3][:, 0:512], PB[3][:, 512:1024]]
        a2 = ag2_out[l].ap().rearrange("(c h2 s d) t -> c h2 s d t", c=4, h2=2, s=8, d=64)
        sg_v = sg_scr[l].ap().rearrange("(c p) t -> p c t", p=128)
        wo_v = w_out[l].rearrange("(c p) n -> p c n", p=128)
        for q4 in range(4):
            s.dma("sp", PANSq, wo_v[:, :, q4 * 256:(q4 + 1) * 256], writes=["PANS0"], slot="PANS0g")
            s.op("act", lambda e, q4=q4: e.activation(out=WO[:, :, q4 * 256:(q4 + 1) * 256], in_=PANSq, func=AF.Copy),
                 reads=["PANS0"], writes=["WO%d" % (q4 // 2)])
        pa = [0]
        tmpi = [0]
        for tg in range(4):
            t0 = tg * 512
            xr = "X%d" % tg
            s.dma("sp", MG[:, 4:8, :], sg_v[:, :, t0:t0 + 512], writes=["MGs"], slot="MGs")
            for sh in range(8):
                tb = tmpi[0] % 5
                tmpi[0] += 1
                for h2 in range(2):
                    s.dma("sp", TMP[tb][h2 * 64:(h2 + 1) * 64, :, :],
                          a2[:, h2, sh, :, t0:t0 + 512].rearrange("c d t -> d c t"),
                          writes=["TMP%d_%d" % (tb, h2)], slot="TMP%d_%d" % (tb, h2))
                tr = ["TMP%d_0" % tb, "TMP%d_1" % tb]
                if sh == 0:
                    s.op("dve", lambda e, tb=tb: e.tensor_scalar(out=AT[:, :, :], in0=TMP[tb][:, :, :],
                                                               scalar1=OH[:, 0:1], scalar2=None, op0=ALU.mult),
                         reads=tr + ["OH"], writes=["AT"])
                else:
                    s.op("dve", lambda e, tb=tb, sh=sh: e.scalar_tensor_tensor(
                        out=AT[:, :, :], in0=TMP[tb][:, :, :], scalar=OH[:, sh:sh + 1], in1=AT[:, :, :],
                        op0=ALU.mult, op1=ALU.add), reads=tr + ["OH", "AT"], writes=["AT"])
            s.op("act", lambda e: e.activation(out=SQ[:, 0:4, :], in_=AT[:, :, :], func=AF.Square),
                 reads=["AT"], writes=["SQ"])
            for c in range(4):
                s.op("pe", lambda e, c=c: e.matmul(P_SS, lhsT=ONES[:, :], rhs=SQ[:, c, :], start=(c == 0),
                                                   stop=(c == 3)), reads=["ONES", "SQ"], writes=["P_SS"])
            s.op("act", lambda e: e.activation(out=RSQ[:, :], in_=P_SS, func=AF.Sqrt, bias=EPSC[:, 0:1],
                                               scale=1.0 / 512), reads=["P_SS", "EPSC"], writes=["RSQ"])
            s.op("dve", lambda e: e.reciprocal(out=RS[:, :], in_=RSQ[:, :]), reads=["RSQ"], writes=["RSd"])
            for c in range(4):
                s.op("dve", lambda e, c=c: e.scalar_tensor_tensor(out=MG[:, c, :], in0=AT[:, c, :],
                                                                scalar=GOA[:, l, c:c + 1], in1=RS[:, :],
                                                                op0=ALU.mult, op1=ALU.mult),
                     reads=["AT", "GOA", "RSd"], writes=["MGa"])
            for oc in range(8):
                pb = pa[0] % 2
                pa[0] += 1
                P = P_A[pb]
                for c in range(8):
                    s.op("pe", lambda e, c=c, oc=oc, P=P: e.matmul(P, lhsT=WO[:, c, oc * 128:(oc + 1) * 128],
                                                                 rhs=MG[:, c, :], start=(c == 0), stop=(c == 7)),
                         reads=["WO%d" % (oc // 4), "MGa", "MGs"], writes=["P_A%d" % pb])
                s.op("dve", lambda e, oc=oc, P=P, t0=t0: e.tensor_tensor(out=X[:, oc, t0:t0 + 512], in0=P,
                                                                       in1=X[:, oc, t0:t0 + 512], op=ALU.add),
                     reads=["P_A%d" % pb, xr], writes=[xr])
        wgu_v = w_gu[l].rearrange("(c p) n -> p c n", p=128)
        wdn_v = w_dn[l].rearrange("(j p) n -> p j n", p=128)
        pan_i = [0]
        dpn_i = [0]
        gi = [0]
        for hf in range(2):
            for grp in range(2):
                tg = hf * 2 + grp
                t0 = tg * 512
                xr = "X%d" % tg
                s.op("act", lambda e, t0=t0: e.activation(out=SQ[:, :, :], in_=X[:, :, t0:t0 + 512], func=AF.Square),
                     reads=[xr], writes=["SQ"])
                for c in range(8):
                    s.op("pe", lambda e, c=c: e.matmul(P_SS, lhsT=ONES[:, :], rhs=SQ[:, c, :], start=(c == 0),
                                                       stop=(c == 7)), reads=["ONES", "SQ"], writes=["P_SS"])
                s.op("act", lambda e: e.activation(out=RSQ[:, :], in_=P_SS, func=AF.Sqrt, bias=EPSC[:, 0:1],
                                                   scale=1.0 / D), reads=["P_SS", "EPSC"], writes=["RSQ"])
                s.op("dve", lambda e, grp=grp: e.reciprocal(out=RSTD[:, grp * 512:(grp + 1) * 512], in_=RSQ[:, :]),
                     reads=["RSQ"], writes=["RSTD%d" % grp])
                for c in range(8):
                    eng = "dve" if c % 2 == 0 else "pool"
                    s.op(eng, lambda e, c=c, t0=t0, grp=grp: e.tensor_scalar(
                        out=XB[:, c, grp * 512:(grp + 1) * 512], in0=X[:, c, t0:t0 + 512],
                        scalar1=GFFN[:, l, c:c + 1], scalar2=None, op0=ALU.mult), reads=[xr, "GFFN"],
                         writes=["XB%d" % grp] + (["MGa", "MGs"] if grp == 0 else []))
            for j in range(22):
                pi = pan_i[0] % 2
                pan_i[0] += 1
                fence = ["AT"] if (pi == 1 and j == 1 and hf == 0) else []
                s.dma("sp", PANSv[pi][:, :, 0, :], wgu_v[:, :, j * 128:(j + 1) * 128], writes=["PANS%d" % pi] + fence,
                      slot="PANS%dg" % pi)
                s.dma("sp", PANSv[pi][:, :, 1, :], wgu_v[:, :, DFF + j * 128:DFF + (j + 1) * 128],
                      writes=["PANS%du" % pi], slot="PANS%du" % pi)
                s.op("act", lambda e, pi=pi: e.activation(out=PAN[pi][:, :], in_=PANS[pi][:, :], func=AF.Copy),
                     reads=["PANS%d" % pi, "PANS%du" % pi], writes=["PANg%d" % pi, "PANu%d" % pi])
                for grp in range(2):
                    gb = gi[0] % 2
                    gi[0] += 1
                    for c in range(8):
                        s.op("pe", lambda e, c=c, pi=pi, grp=grp, gb=gb: e.matmul(
                            P_G[gb], lhsT=PANv[pi][:, c, 0, :], rhs=XB[:, c, grp * 512:(grp + 1) * 512],
                            start=(c == 0), stop=(c == 7)), reads=["PANg%d" % pi, "XB%d" % grp],
                            writes=["P_G%d" % gb])
                    for c in range(8):
                        s.op("pe", lambda e, c=c, pi=pi, grp=grp, gb=gb: e.matmul(
                            P_U[gb], lhsT=PANv[pi][:, c, 1, :], rhs=XB[:, c, grp * 512:(grp + 1) * 512],
                            start=(c == 0), stop=(c == 7)), reads=["PANu%d" % pi, "XB%d" % grp],
                            writes=["P_U%d" % gb])
                    rs = RSTD[:, grp * 512:(grp + 1) * 512]
                    s.op("dve", lambda e, gb=gb, rs=rs: e.tensor_tensor(out=G1[gb][:, :], in0=P_G[gb], in1=rs,
                                                                      op=ALU.mult),
                         reads=["P_G%d" % gb, "RSTD%d" % grp], writes=["G1_%d" % gb])
                    s.op("act", lambda e, gb=gb: e.activation(out=G1[gb][:, :], in_=G1[gb][:, :], func=AF.Silu),
                         reads=["G1_%d" % gb], writes=["G1_%d" % gb])
                    s.op("dve", lambda e, gb=gb, rs=rs: e.tensor_tensor(out=U1[gb][:, :], in0=P_U[gb], in1=rs,
                                                                      op=ALU.mult),
                         reads=["P_U%d" % gb, "RSTD%d" % grp], writes=["U1_%d" % gb])
                    s.op("pool", lambda e, gb=gb, j=j, grp=grp: e.tensor_tensor(
                        out=ACTT[:, j, grp * 512:(grp + 1) * 512], in0=G1[gb][:, :], in1=U1[gb][:, :], op=ALU.mult),
                        reads=["G1_%d" % gb, "U1_%d" % gb], writes=["ACTT%d" % grp])
            for oc in range(8):
                di = dpn_i[0] % 2
                dpn_i[0] += 1
                for hh in range(2):
                    fence = ["WO0", "WO1"] if (hf == 0 and oc == 0) else []
                    s.dma("sp", DPNS[hh][:, :, :], wdn_v[:, hh * 11:(hh + 1) * 11, oc * 128:(oc + 1) * 128],
                          writes=["DPNS%d" % hh] + fence, slot="DPNS%d" % hh)
                    s.op("pool", lambda e, di=di, hh=hh: e.tensor_copy(out=DPN[di][:, hh * 11:(hh + 1) * 11, :],
                                                                     in_=DPNS[hh][:, :, :]),
                         reads=["DPNS%d" % hh], writes=["DPN%d_%d" % (di, hh)])
                for grp in range(2):
                    tg = hf * 2 + grp
                    t0 = tg * 512
                    xr = "X%d" % tg
                    pb = pa[0] % 2
                    pa[0] += 1
                    P = P_A[pb]
                    for j in range(22):
                        s.op("pe", lambda e, j=j, di=di, grp=grp, P=P: e.matmul(
                            P, lhsT=DPN[di][:, j, :], rhs=ACTT[:, j, grp * 512:(grp + 1) * 512],
                            start=(j == 0), stop=(j == 21)), reads=["DPN%d_0" % di, "DPN%d_1" % di, "ACTT%d" % grp],
                            writes=["P_A%d" % pb])
                    s.op("dve", lambda e, oc=oc, P=P, t0=t0: e.tensor_tensor(out=X[:, oc, t0:t0 + 512], in0=P,
                                                                           in1=X[:, oc, t0:t0 + 512], op=ALU.add),
                         reads=["P_A%d" % pb, xr], writes=[xr])
        if not last:
            return
        yo_v = yo.rearrange("(c p) t -> p c t", p=128)
        for tg in range(4):
            t0 = tg * 512
            xr = "X%d" % tg
            s.op("act", lambda e, t0=t0: e.activation(out=SQ[:, :, :], in_=X[:, :, t0:t0 + 512], func=AF.Square),
                 reads=[xr], writes=["SQ"])
            for c in range(8):
                s.op("pe", lambda e, c=c: e.matmul(P_SS, lhsT=ONES[:, :], rhs=SQ[:, c, :], start=(c == 0),
                                                   stop=(c == 7)), reads=["ONES", "SQ"], writes=["P_SS"])
            s.op("act", lambda e: e.activation(out=RSQ[:, :], in_=P_SS, func=AF.Sqrt, bias=EPSC[:, 0:1],
                                               scale=1.0 / D), reads=["P_SS", "EPSC"], writes=["RSQ"])
            s.op("dve", lambda e: e.reciprocal(out=RS[:, :], in_=RSQ[:, :]), reads=["RSQ"], writes=["RSd"])
            for c in range(8):
                yb = c % 2
                s.op("dve", lambda e, c=c, t0=t0, yb=yb: e.scalar_tensor_tensor(
                    out=G1[yb][:, :], in0=X[:, c, t0:t0 + 512], scalar=GFIN[:, c:c + 1], in1=RS[:, :],
                    op0=ALU.mult, op1=ALU.mult), reads=[xr, "GFIN", "RSd"], writes=["G1_%d" % yb])
                s.dma("sp", yo_v[:, c, t0:t0 + 512], G1[yb][:, :], reads=["G1_%d" % yb], writes=["o_y"],
                      slot="YS%d" % yb)

    def dbg_out():
        yo_v = yo.rearrange("(c p) t -> p c t", p=128)
        for tg in range(4):
            s.dma("sp", yo_v[:, :, tg * 512:(tg + 1) * 512], X[:, :, tg * 512:(tg + 1) * 512], reads=["X%d" % tg],
                  writes=["o_y%d" % tg], slot="dbgo%d" % tg)

    for l in range(depth):
        outs = phase_A(l)
        if stop == 'A':
            dbg_out(); break
        s.cc(ag1_in[l], ag1_out[l], reads=[o for o in outs if o.startswith("oq")], writes=["ag1o"], slot="cc1")
        s.cc(agf_in[l], agf_out[l], reads=[o for o in outs if o.startswith("of")], writes=["agfo"], slot="ccf")
        s.barrier()
        if stop == 'Acc':
            dbg_out(); break
        if stop in ('B0', 'B1'):
            phase_B(l, int(stop[1])); s.barrier(); dbg_out(); break
        onames = phase_B(l)
        if stop == 'B':
            dbg_out(); break
        s.cc(ag2_in[l], ag2_out[l], reads=onames, writes=["ag2o"], slot="cc2")
        s.barrier()
        if stop == 'Bcc':
            dbg_out(); break
        phase_C(l, l == depth - 1)
        s.barrier()
    s.emit()
    return nc


_NC = {}


def _pc(vec, nchunk):
    return np.ascontiguousarray(np.asarray(vec, np.float32).reshape(nchunk, 128).T)


def _consts():
    identf = np.eye(128, dtype=np.float32)
    identb = identf.astype(ml_dtypes.bfloat16)
    tri = (np.arange(128)[:, None] < np.arange(128)[None, :]).astype(np.float32)
    p = np.arange(128)[:, None, None]
    kbd = np.arange(4)[None, :, None]
    j = np.arange(512)[None, None, :]
    maskneg = np.where(kbd * 128 + p > j, NEG, 0.0).astype(np.float32).astype(ml_dtypes.bfloat16)
    return identf, identb, tri, maskneg


def _host_inputs(x, mix_norm_g, w_in, b_f, sgu_ln_g, sgu_ln_b, w_s, b_s, out_norm_g, w_out,
                 ffn_norm_g, w_gate_up, w_down, final_norm_g, depth):
    f32 = np.float32
    identf, identb, tri, maskneg = _consts()
    x = np.asarray(x, f32)
    L = depth
    ong = np.asarray(out_norm_g, f32)
    shared = dict(
        gmix=np.ascontiguousarray(np.stack([_pc(mix_norm_g[l], 8) for l in range(L)], axis=1)),
        w_in=np.ascontiguousarray(np.asarray(w_in, f32)[:L]),
        lng=np.ascontiguousarray(np.broadcast_to(np.asarray(sgu_ln_g, f32)[:L, None, :], (L, 128, 512))),
        lnb=np.ascontiguousarray(np.broadcast_to(np.asarray(sgu_ln_b, f32)[:L, None, :], (L, 128, 512))),
        wsT=np.ascontiguousarray(np.transpose(np.asarray(w_s, f32)[:L], (0, 3, 1, 2))),
        bsb=np.ascontiguousarray(np.broadcast_to(np.asarray(b_s, f32)[:L].reshape(L, 4, 2, 1, 128),
                                                 (L, 4, 2, 64, 128)).transpose(0, 2, 3, 1, 4).reshape(L, 128, 4, 128)),
        gos=np.ascontiguousarray(np.stack([_pc(ong[l, 512:], 4) for l in range(L)], axis=1)),
        goa=np.ascontiguousarray(np.stack([_pc(ong[l, :512], 4) for l in range(L)], axis=1)),
        gffn=np.ascontiguousarray(np.stack([_pc(ffn_norm_g[l], 8) for l in range(L)], axis=1)),
        gfin=_pc(final_norm_g, 8),
        w_out=np.ascontiguousarray(np.asarray(w_out, f32)[:L]),
        w_gu=np.ascontiguousarray(np.asarray(w_gate_up, f32)[:L]),
        w_dn=np.ascontiguousarray(np.asarray(w_down, f32)[:L]),
        identf=identf, identb=identb, tri=tri, maskneg=maskneg)
    in_maps = []
    bfa = np.asarray(b_f, f32)
    for r in range(NCORES):
        m = dict(shared)
        m["xT"] = np.ascontiguousarray(x[0, r * TS:(r + 1) * TS, :].T)
        m["bfh"] = np.ascontiguousarray(np.broadcast_to(bfa[:L, r][None, :], (128, L)))
        selm = np.zeros((4, 128, 64), f32)
        for d in range(64):
            feat = r * 64 + d
            selm[feat // 128, feat % 128, d] = 1.0
        m["sel"] = np.ascontiguousarray(selm.transpose(1, 0, 2)).astype(ml_dtypes.bfloat16)
        ohm = np.zeros((128, 8), f32)
        ohm[:, r] = 1.0
        m["oh"] = ohm
        in_maps.append(m)
    return in_maps


def kernel(x, mix_norm_g, w_in, b_f, sgu_ln_g, sgu_ln_b, w_s, b_s, out_norm_g, w_out,
           ffn_norm_g, w_gate_up, w_down, final_norm_g, _depth=DEPTH, _stop='full'):
    if (_depth, _stop) not in _NC:
        _NC[(_depth, _stop)] = build_fused(_depth, _stop)
    in_maps = _host_inputs(x, mix_norm_g, w_in, b_f, sgu_ln_g, sgu_ln_b, w_s, b_s, out_norm_g, w_out,
                           ffn_norm_g, w_gate_up, w_down, final_norm_g, _depth)
    if _stop not in ('full', 'C'):
        for m in in_maps:
            for k in ("w_out", "w_gu", "w_dn"):
                m.pop(k)
    res = run_bass_kernel_spmd(_NC[(_depth, _stop)], in_maps, core_ids=list(range(NCORES))).results
    out = np.concatenate([np.asarray(res[r]["yo"]).T for r in range(NCORES)], axis=0)[None].astype(np.float32)
    return out
```
